# Optimizing a Trainium2 kernel written in Bass

```python
import jax
import jax.numpy as jnp
from jax import lax
import numpy as np

D_MODEL = 1024
BATCH = 8
SEQ = 4096
DEPTH = 4

GRID_W = 64
CTX_LEN = 256
HEAD_DIM = 64
D_MIX = D_MODEL
D_MLSTM = D_MIX // 2
D_NA = D_MIX - D_MLSTM
H_MLSTM = D_MLSTM // HEAD_DIM
H_NA = D_NA // HEAD_DIM
N_GATE = 4 * H_MLSTM
D_IN = 4 * D_MLSTM + 3 * D_NA + N_GATE
MLSTM_CHUNK = 64
NA_ROWS_MAX = 8
NA_COLS = 16
D_FF = 4 * D_MODEL
N_MOD = 6
ROPE_BASE = 10000.0
EPS = 1e-6
F_BIAS_LO = 3.0
F_BIAS_HI = 6.0

kernel_name = "hybrid_mlstm_natten_dit_block"


def rmsnorm(x, g):
    x32 = x.astype(jnp.float32)
    y = x32 * lax.rsqrt(jnp.mean(x32 * x32, axis=-1, keepdims=True) + EPS)
    return (y * g.astype(jnp.float32)).astype(x.dtype)


def sq_relu_mlp(h, w1, w2):
    return jnp.square(jax.nn.relu(h @ w1)) @ w2


def split_heads(a, n_heads):
    b, t, _ = a.shape
    return a.reshape(b, t, n_heads, HEAD_DIM).transpose(0, 2, 1, 3)


def merge_heads(a):
    b, h, t, d = a.shape
    return a.transpose(0, 2, 1, 3).reshape(b, t, h * d)


def head_layernorm(h, g):
    mu = jnp.mean(h, axis=-1, keepdims=True)
    var = jnp.mean(jnp.square(h - mu), axis=-1, keepdims=True)
    return merge_heads((h - mu) * lax.rsqrt(var + EPS)) * g.astype(jnp.float32)


def axial_rope_tables(n_tokens):
    t = jnp.arange(n_tokens)
    row = (t // GRID_W).astype(jnp.float32)
    col = (t % GRID_W).astype(jnp.float32)
    n_freq = HEAD_DIM // 4
    inv_freq = ROPE_BASE ** (-jnp.arange(n_freq, dtype=jnp.float32) / n_freq)
    ang = jnp.concatenate([row[:, None] * inv_freq, col[:, None] * inv_freq], axis=-1)
    return jnp.cos(ang), jnp.sin(ang)


def apply_rope(x, cos, sin):
    x1, x2 = jnp.split(x, 2, axis=-1)
    return jnp.concatenate([x1 * cos - x2 * sin, x1 * sin + x2 * cos], axis=-1)


def mlstm_scan(q, k, v, log_i, log_f, state):
    B, H, T, d = q.shape
    L = MLSTM_CHUNK
    nc = T // L

    def chunks(a):
        return jnp.moveaxis(a.reshape(a.shape[:2] + (nc, L) + a.shape[3:]), 2, 0)

    causal = jnp.tril(jnp.ones((L, L), dtype=bool))

    def step(carry, xs):
        C, n, m = carry
        qc, kc, vc, ic, fc = xs
        b = jnp.cumsum(fc, axis=-1)
        d_intra = jnp.where(causal, b[..., :, None] - b[..., None, :] + ic[..., None, :], -jnp.inf)
        d_prev = b + m[..., None]
        m_t = jnp.maximum(d_prev, jnp.max(d_intra, axis=-1))
        w_intra = jnp.exp(d_intra - m_t[..., None])
        w_prev = jnp.exp(d_prev - m_t)
        s = jnp.einsum('bhtd,bhsd->bhts', qc, kc) * w_intra
        num = w_prev[..., None] * jnp.einsum('bhtd,bhde->bhte', qc, C) + jnp.einsum('bhts,bhse->bhte', s, vc)
        qn = w_prev * jnp.einsum('bhtd,bhd->bht', qc, n) + jnp.sum(s, axis=-1)
        h = num / jnp.maximum(jnp.abs(qn), jnp.exp(-m_t))[..., None]
        b_end = b[..., -1]
        d_end = b_end[..., None] - b + ic
        m_new = jnp.maximum(b_end + m, jnp.max(d_end, axis=-1))
        w_c = jnp.exp(b_end + m - m_new)
        w_s = jnp.exp(d_end - m_new[..., None])
        C = w_c[..., None, None] * C + jnp.einsum('bhs,bhsd,bhse->bhde', w_s, kc, vc)
        n = w_c[..., None] * n + jnp.einsum('bhs,bhsd->bhd', w_s, kc)
        return (C, n, m_new), h

    state, h = lax.scan(step, state, (chunks(q), chunks(k), chunks(v), chunks(log_i), chunks(log_f)))
    return jnp.moveaxis(h, 0, 2).reshape(B, H, T, d), state


def mlstm_group(q, k, v, o, gates, qc, kc, vc, oc, gates_c, b_gate, norm_g, cos, sin, need_ctx):
    f32 = jnp.float32
    scale = HEAD_DIM ** -0.5

    def prep(a):
        return split_heads(a, H_MLSTM).astype(f32)

    qx = apply_rope(prep(q), cos, sin) * scale
    kx = apply_rope(prep(k), cos, sin)
    vx = prep(v)
    qcx = prep(qc) * scale
    kcx = prep(kc)
    vcx = prep(vc)

    def gate_split(g):
        return jnp.split((g.astype(f32) + b_gate.astype(f32)).transpose(0, 2, 1), 4, axis=1)

    i_fw, i_bw, f_fw, f_bw = gate_split(gates)
    ic_fw, ic_bw, fc_fw, fc_bw = gate_split(gates_c)
    B, H, _, d = qx.shape
    state0 = (jnp.zeros((B, H, d, d), f32), jnp.zeros((B, H, d), f32), jnp.zeros((B, H), f32))

    def direction(rev, ig, fg, igc, fgc):
        flip = (lambda a: jnp.flip(a, axis=2)) if rev else (lambda a: a)
        hc, st = mlstm_scan(flip(qcx), flip(kcx), flip(vcx), flip(igc), jax.nn.log_sigmoid(flip(fgc)), state0)
        hx, _ = mlstm_scan(flip(qx), flip(kx), flip(vx), flip(ig), jax.nn.log_sigmoid(flip(fg)), st)
        return flip(hx), flip(hc)

    hx_f, hc_f = direction(False, i_fw, f_fw, ic_fw, fc_fw)
    hx_b, hc_b = direction(True, i_bw, f_bw, ic_bw, fc_bw)
    y = (head_layernorm(hx_f + hx_b, norm_g) * jax.nn.sigmoid(o.astype(f32))).astype(q.dtype)
    yc = None
    if need_ctx:
        yc = (head_layernorm(hc_f + hc_b, norm_g) * jax.nn.sigmoid(oc.astype(f32))).astype(q.dtype)
    return y, yc


def neighborhood_attention(q, k, v, qc, kc, vc, rpb, rows, need_ctx):
    B, S, _ = q.shape
    scale = HEAD_DIM ** -0.5
    kr_n = min(NA_ROWS_MAX, rows)

    def grid(a):
        return split_heads(a, H_NA).reshape(B, H_NA, rows, GRID_W, HEAD_DIM)

    qg = grid(q * scale)
    kg = grid(k)
    vg = grid(v)
    kch = split_heads(kc, H_NA)
    vch = split_heads(vc, H_NA)
    cols = jnp.arange(GRID_W)
    col_start = jnp.clip(cols - NA_COLS // 2, 0, GRID_W - NA_COLS)
    col_mask = (cols[None, :] >= col_start[:, None]) & (cols[None, :] < col_start[:, None] + NA_COLS)
    dc_idx = jnp.clip(cols[None, :] - cols[:, None] + NA_COLS - 1, 0, 2 * NA_COLS - 2)
    rpb_cols = rpb.astype(jnp.float32)[:, :, dc_idx]

    def row_block(r):
        rs = jnp.clip(r - kr_n // 2, 0, rows - kr_n)
        q_r = lax.dynamic_index_in_dim(qg, r, axis=2, keepdims=False)
        k_r = lax.dynamic_slice_in_dim(kg, rs, kr_n, axis=2)
        v_r = lax.dynamic_slice_in_dim(vg, rs, kr_n, axis=2)
        bias = jnp.take(rpb_cols, rs + jnp.arange(kr_n) - r + NA_ROWS_MAX - 1, axis=1)
        s_loc = jnp.einsum('bhqd,bhrkd->bhqrk', q_r, k_r).astype(jnp.float32) + bias.transpose(0, 2, 1, 3)
        s_loc = jnp.where(col_mask[:, None, :], s_loc, -jnp.inf)
        s_ctx = jnp.einsum('bhqd,bhcd->bhqc', q_r, kch).astype(jnp.float32)
        s = jnp.concatenate([s_loc.reshape(B, H_NA, GRID_W, kr_n * GRID_W), s_ctx], axis=-1)
        p = jax.nn.softmax(s, axis=-1).astype(v.dtype)
        p_loc = p[..., :kr_n * GRID_W].reshape(B, H_NA, GRID_W, kr_n, GRID_W)
        p_ctx = p[..., kr_n * GRID_W:]
        return jnp.einsum('bhqrk,bhrkd->bhqd', p_loc, v_r) + jnp.einsum('bhqc,bhcd->bhqd', p_ctx, vch)

    out = lax.map(row_block, jnp.arange(rows))
    y = out.transpose(1, 0, 3, 2, 4).reshape(B, S, H_NA * HEAD_DIM)
    yc = None
    if need_ctx:
        qch = split_heads(qc * scale, H_NA)
        sc = jnp.einsum('bhqd,bhkd->bhqk', qch, kch).astype(jnp.float32)
        pc = jax.nn.softmax(sc, axis=-1).astype(vc.dtype)
        yc = merge_heads(jnp.einsum('bhqk,bhkd->bhqd', pc, vch))
    return y, yc


def token_mixers(p, pc, b_gate, norm_g, rpb, cos, sin, rows, need_ctx):
    cuts = [int(i) for i in np.cumsum([D_MLSTM] * 4 + [D_NA] * 3)]
    qm, km, vm, om, qn, kn, vn, gm = jnp.split(p, cuts, axis=-1)
    qmc, kmc, vmc, omc, qnc, knc, vnc, gmc = jnp.split(pc, cuts, axis=-1)
    y_m, yc_m = mlstm_group(qm, km, vm, om, gm, qmc, kmc, vmc, omc, gmc, b_gate, norm_g, cos, sin, need_ctx)
    y_n, yc_n = neighborhood_attention(qn, kn, vn, qnc, knc, vnc, rpb, rows, need_ctx)
    y = jnp.concatenate([y_m, y_n], axis=-1)
    yc = jnp.concatenate([yc_m, yc_n], axis=-1) if need_ctx else None
    return y, yc


def setup_inputs(seed: int = 0) -> dict:
    key = jax.random.key(seed)
    ks = jax.random.split(key, 16)
    nrm = jax.random.normal
    f32 = jnp.float32
    x = nrm(ks[0], (BATCH, SEQ, D_MODEL), f32)
    c = nrm(ks[1], (BATCH, D_MODEL), f32)
    ctx = nrm(ks[2], (BATCH, CTX_LEN, D_MODEL), f32)
    c_ctx = nrm(ks[3], (D_MODEL,), f32)
    w_ada = nrm(ks[4], (DEPTH, D_MODEL, N_MOD * D_MODEL), f32) * (0.5 * D_MODEL ** -0.5)
    b_ada = 0.02 * nrm(ks[5], (DEPTH, N_MOD * D_MODEL), f32)
    norm1_g = 1.0 + 0.02 * nrm(ks[6], (DEPTH, D_MODEL), f32)
    w_in = nrm(ks[7], (DEPTH, D_MODEL, D_IN), f32) * D_MODEL ** -0.5
    i_bias = 0.1 * nrm(ks[8], (DEPTH, 2 * H_MLSTM), f32)
    f_base = jnp.tile(jnp.linspace(F_BIAS_LO, F_BIAS_HI, H_MLSTM, dtype=f32), 2)
    f_bias = f_base[None, :] + 0.1 * nrm(ks[9], (DEPTH, 2 * H_MLSTM), f32)
    b_gate = jnp.concatenate([i_bias, f_bias], axis=-1)
    mlstm_norm_g = 1.0 + 0.02 * nrm(ks[10], (DEPTH, D_MLSTM), f32)
    rpb = 0.1 * nrm(ks[11], (DEPTH, H_NA, 2 * NA_ROWS_MAX - 1, 2 * NA_COLS - 1), f32)
    w_out = nrm(ks[12], (DEPTH, D_MIX, D_MODEL), f32) * D_MIX ** -0.5
    norm2_g = 1.0 + 0.02 * nrm(ks[13], (DEPTH, D_MODEL), f32)
    km1, km2 = jax.random.split(ks[14])
    w_mlp1 = nrm(km1, (DEPTH, D_MODEL, D_FF), f32) * D_MODEL ** -0.5
    w_mlp2 = nrm(km2, (DEPTH, D_FF, D_MODEL), f32) * D_FF ** -0.5
    final_g = 1.0 + 0.02 * nrm(ks[15], (D_MODEL,), f32)
    return {"x": x, "c": c, "ctx": ctx, "c_ctx": c_ctx, "w_ada": w_ada, "b_ada": b_ada,
            "norm1_g": norm1_g, "w_in": w_in, "b_gate": b_gate, "mlstm_norm_g": mlstm_norm_g,
            "rpb": rpb, "w_out": w_out, "norm2_g": norm2_g, "w_mlp1": w_mlp1, "w_mlp2": w_mlp2,
            "final_g": final_g}


def reference(x, c, ctx, c_ctx, w_ada, b_ada, norm1_g, w_in, b_gate, mlstm_norm_g, rpb, w_out,
              norm2_g, w_mlp1, w_mlp2, final_g):
    B, S, D = x.shape
    rows = S // GRID_W
    cos, sin = axial_rope_tables(S)
    silu_c = jax.nn.silu(c)
    silu_cc = jax.nn.silu(c_ctx)
    xc = ctx
    for l in range(DEPTH):
        need_ctx = l < DEPTH - 1
        mod = silu_c @ w_ada[l] + b_ada[l]
        mod_c = silu_cc @ w_ada[l] + b_ada[l]
        sh1, sc1, g1, sh2, sc2, g2 = jnp.split(mod[:, None, :], N_MOD, axis=-1)
        sh1c, sc1c, g1c, sh2c, sc2c, g2c = jnp.split(mod_c, N_MOD, axis=-1)
        h = rmsnorm(x, norm1_g[l]) * (1.0 + sc1) + sh1
        hc = rmsnorm(xc, norm1_g[l]) * (1.0 + sc1c) + sh1c
        y, yc = token_mixers(h @ w_in[l], hc @ w_in[l], b_gate[l], mlstm_norm_g[l], rpb[l],
                             cos, sin, rows, need_ctx)
        x = x + g1 * (y @ w_out[l])
        h = rmsnorm(x, norm2_g[l]) * (1.0 + sc2) + sh2
        x = x + g2 * sq_relu_mlp(h, w_mlp1[l], w_mlp2[l])
        if need_ctx:
            xc = xc + g1c * (yc @ w_out[l])
            hc = rmsnorm(xc, norm2_g[l]) * (1.0 + sc2c) + sh2c
            xc = xc + g2c * sq_relu_mlp(hc, w_mlp1[l], w_mlp2[l])
    return rmsnorm(x, final_g)
```

```python
import os
import numpy as np
from contextlib import ExitStack
import concourse.bass as bass
import concourse.mybir as mybir
from concourse.bass_utils import run_bass_kernel_spmd

F32 = mybir.dt.float32
BF16 = mybir.dt.bfloat16
AF = mybir.ActivationFunctionType
ALU = mybir.AluOpType
AX = mybir.AxisListType


class Prog:
    ENGS = ("pe", "dve", "act", "pool", "sp")
    EPOCH = 12000
    NSLOT = 10

    def __init__(self, nc):
        self.nc = nc
        self.stack = ExitStack()
        self.ops = []
        self.hist = {}
        self.flag = []

    def sb(self, name, shape, dtype):
        return self.stack.enter_context(self.nc.sbuf_tensor(name, list(shape), dtype))

    def ps(self, name, shape, dtype):
        return self.stack.enter_context(self.nc.psum_tensor(name, list(shape), dtype))

    @staticmethod
    def region(ap):
        t = ap.tensor
        name = t.name
        sz = 4 if ap.dtype == F32 else 2
        dims = [(int(s) * sz, int(c)) for s, c in ap.ap]
        off = int(ap.offset) * sz
        if "DRam" in type(t).__name__:
            lo = off + sum(min(0, s * (c - 1)) for s, c in dims)
            hi = off + sum(max(0, s * (c - 1)) for s, c in dims) + sz
            return (name, 0, 1, lo, hi)
        pstride = sz
        for d in t.shape[1:]:
            pstride *= int(d)
        ps, pc = dims[0]
        if ps == 0:
            pc = 1
        p0 = off // pstride
        base = off % pstride
        rest = dims[1:]
        lo = base + sum(min(0, s * (c - 1)) for s, c in rest)
        hi = base + sum(max(0, s * (c - 1)) for s, c in rest) + sz
        p1 = p0 + pc
        if "PSum" in type(t).__name__:
            lo = lo // 2048 * 2048
            hi = (hi + 2047) // 2048 * 2048
            p0, p1 = 0, 128
        return (name, p0, p1, lo, hi)

    def _deps_and_update(self, oid, eng, is_dma, reads, writes):
        deps = set()
        rregs = [self.region(a) if not isinstance(a, tuple) else a for a in reads]
        wregs = [self.region(a) if not isinstance(a, tuple) else a for a in writes]
        for (name, p0, p1, lo, hi) in rregs:
            recs = self.hist.get(name)
            if recs is None:
                recs = self.hist[name] = []
            hit = False
            for r in recs:
                if r[0] < p1 and p0 < r[1] and r[2] < hi and lo < r[3]:
                    if r[4] is not None:
                        deps.add(r[4])
                    if r[0] <= p0 and p1 <= r[1] and r[2] <= lo and hi <= r[3]:
                        hit = True
                    self._add_reader(r, oid, eng, is_dma)
            if not hit:
                r = [p0, p1, lo, hi, None, {}, []]
                self._add_reader(r, oid, eng, is_dma)
                recs.append(r)
        for (name, p0, p1, lo, hi) in wregs:
            recs = self.hist.get(name)
            if recs is None:
                recs = self.hist[name] = []
            keep = []
            for r in recs:
                if r[0] < p1 and p0 < r[1] and r[2] < hi and lo < r[3]:
                    if r[4] is not None:
                        deps.add(r[4])
                    deps.update(r[5].values())
                    deps.update(r[6])
                    if p0 <= r[0] and r[1] <= p1 and lo <= r[2] and r[3] <= hi:
                        continue
                keep.append(r)
            keep.append([p0, p1, lo, hi, oid, {}, []])
            self.hist[name] = keep
        deps.discard(oid)
        return deps

    @staticmethod
    def _add_reader(r, oid, eng, is_dma):
        if is_dma:
            r[6].append(oid)
            if len(r[6]) > 64:
                del r[6][0:len(r[6]) - 64]
        else:
            r[5][eng] = oid

    def op(self, eng, fn, reads, writes, dma=False):
        oid = len(self.ops)
        extra = [a for a in reads if not isinstance(a, tuple) and "PSum" in type(a.tensor).__name__]
        if extra:
            writes = list(writes) + extra
        deps = self._deps_and_update(oid, eng, dma, reads, writes)
        self.ops.append((eng, fn, deps, dma))
        return oid

    def dma(self, q, out, in_, **kw):
        return self.op(q, lambda e: e.dma_start(out=out, in_=in_, **kw), [in_], [out], dma=True)

    def finish(self):
        nc = self.nc
        ops = self.ops
        n = len(ops)
        flagged = [False] * n
        for (eng, fn, deps, dma) in ops:
            for d in deps:
                de = ops[d][0]
                if ops[d][3]:
                    continue
                if de == "pe" and eng == "pe" and not dma:
                    continue
                flagged[d] = True
        token = [None] * n
        cnt = {e: 0 for e in self.ENGS}
        dcnt = {e: 0 for e in self.ENGS}
        for i, (eng, fn, deps, dma) in enumerate(ops):
            if dma:
                k = dcnt[eng]
                dcnt[eng] += 1
                token[i] = (("d", eng, k % self.NSLOT), 16 * (k // self.NSLOT + 1))
            elif flagged[i]:
                k = cnt[eng]
                cnt[eng] += 1
                token[i] = (("c", eng, k // self.EPOCH), k % self.EPOCH + 1)
        sems = {}
        for tk in token:
            if tk is not None and tk[0] not in sems:
                sems[tk[0]] = None
        for key in list(sems.keys()):
            sems[key] = self.stack.enter_context(nc.semaphore("s_" + "_".join(str(x) for x in key)))
        per_eng = {e: [] for e in self.ENGS}
        for i, o in enumerate(ops):
            per_eng[o[0]].append(i)
        self.n_wait = 0
        if os.environ.get("KSTATS"):
            print("KSTATS ops", {e: len(v) for e, v in per_eng.items()}, "flagged", cnt, "dmas", dcnt, "sems", len(sems), flush=True)

        def emit(engname, e):
            seen = {}
            dslot_last = {}
            for i in per_eng[engname]:
                (eng, fn, deps, dma) = ops[i]
                need = {}
                for d in deps:
                    if (not ops[d][3]) and ops[d][0] == "pe" and eng == "pe" and not dma:
                        continue
                    key, val = token[d]
                    if seen.get(key, 0) >= val:
                        continue
                    if need.get(key, 0) < val:
                        need[key] = val
                if dma:
                    key, val = token[i]
                    if val > 16 and seen.get(key, 0) < val - 16:
                        if need.get(key, 0) < val - 16:
                            need[key] = val - 16
                for key, val in need.items():
                    e.wait_ge(sems[key], val)
                    seen[key] = val
                    self.n_wait += 1
                ins = fn(e)
                if dma:
                    key, val = token[i]
                    ins.then_inc(sems[key], 16)
                    dslot_last[key] = val
                elif token[i] is not None:
                    ins.then_inc(sems[token[i][0]], 1)
            return seen, dslot_last

        with nc.Block() as block:
            @block.tensor
            def _(e):
                emit("pe", e)

            @block.vector
            def _(e):
                emit("dve", e)

            @block.scalar
            def _(e):
                seen, last = emit("act", e)
                for key, val in last.items():
                    if seen.get(key, 0) < val:
                        e.wait_ge(sems[key], val)

            @block.gpsimd
            def _(e):
                seen, last = emit("pool", e)
                for key, val in last.items():
                    if seen.get(key, 0) < val:
                        e.wait_ge(sems[key], val)

            @block.sync
            def _(e):
                seen, last = emit("sp", e)
                for key, val in last.items():
                    if seen.get(key, 0) < val:
                        e.wait_ge(sems[key], val)
        self.stack.close()


import math

D = 1024
S = 4096
CTX = 256
DEPTH = 4
NT = 34
DIN = 3616
DFF = 4096
EPS = 1e-6
NPAT = 21
NEG = -30000.0


def na_patterns():
    pats = []
    keys = {}

    def pid(jc, dl):
        k = (jc, dl)
        if k not in pats:
            pats.append(k)
        return pats.index(k)

    for j in range(32):
        if j == 0:
            jc, dls = "b0", [0, 1, 2, 3]
        elif j == 1:
            jc, dls = "b1", [-1, 0, 1, 2]
        elif j == 30:
            jc, dls = "b30", [-2, -1, 0, 1]
        elif j == 31:
            jc, dls = "b31", [-3, -2, -1, 0]
        else:
            jc, dls = "in", [-2, -1, 0, 1, 2]
        keys[j] = [(j + dl, pid(jc, dl)) for dl in dls]
    return pats, keys


def host_constants():
    cst = np.zeros((128, 640), np.float32)
    cst[:, 0:128] = np.eye(128, dtype=np.float32)
    J = np.zeros((128, 128), np.float32)
    for p in range(128):
        J[p, (p // 64) * 64 + 63 - (p % 64)] = 1.0
    cst[:, 128:256] = J
    s_ = np.arange(128)[:, None]
    t_ = np.arange(128)[None, :]
    cst[:, 256:384] = (s_ <= t_).astype(np.float32)
    cst[:, 384:512] = (s_ >= t_).astype(np.float32)
    cst[:, 512:640] = 1.0
    t = np.arange(S)
    row = (t // 64).astype(np.float32)
    col = (t % 64).astype(np.float32)
    n_freq = 16
    inv_freq = (np.float32(10000.0) ** (-np.arange(n_freq, dtype=np.float32) / np.float32(n_freq))).astype(np.float32)
    ang = np.concatenate([row[:, None] * inv_freq, col[:, None] * inv_freq], axis=-1).astype(np.float32)
    cos = np.cos(ang).astype(np.float32).reshape(32, 128, 32).transpose(1, 0, 2)
    sin = np.sin(ang).astype(np.float32).reshape(32, 128, 32).transpose(1, 0, 2)
    rope = np.ascontiguousarray(np.stack([cos, sin], axis=1)).reshape(128, 2 * 32 * 32)
    pats, keys = na_patterns()
    nam = np.full((128, NPAT, 128), NEG, np.float32)
    jrep = {"b0": 0, "b1": 1, "b30": 30, "b31": 31, "in": 10}
    for pi, (jc, dl) in enumerate(pats):
        j = jrep[jc]
        for kp in range(128):
            kr = 2 * (j + dl) + kp // 64
            kc = kp % 64
            for u in range(128):
                qr = 2 * j + u // 64
                qc = 63 - (u % 64)
                rs = min(max(qr - 4, 0), 56)
                cs = min(max(qc - 8, 0), 48)
                if rs <= kr < rs + 8 and cs <= kc < cs + 16:
                    nam[kp, pi, u] = 0.0
    return cst, rope, nam.reshape(128, NPAT * 128)


class Ctx:
    pass


def build_program(depth=DEPTH, dbg=()):
    nc = bass.Bass("TRN2", target_bir_lowering=False)
    P = Prog(nc)
    g = Ctx()
    g.nc, g.P, g.depth = nc, P, depth
    g.nolast = bool(os.environ.get("NOLAST"))

    def din(name, shape, dt=F32):
        return nc.dram_tensor(name, list(shape), dt, kind="ExternalInput").ap()

    def dscr(name, shape, dt):
        if name in dbg:
            return nc.dram_tensor(name, list(shape), dt, kind="ExternalOutput").ap()
        return nc.dram_tensor(name, list(shape), dt).ap()

    g.x = din("x", [S, D])
    g.ctx = din("ctx", [CTX, D])
    g.cc = din("cc", [2, D])
    g.w_ada = din("w_ada", [DEPTH, D, 6 * D])
    g.b_ada = din("b_ada", [DEPTH, 6 * D])
    g.norm1_g = din("norm1_g", [DEPTH, D])
    g.w_in = din("w_in", [DEPTH, D, DIN])
    g.b_gate = din("b_gate", [DEPTH, 32])
    g.mng = din("mlstm_norm_g", [DEPTH, 512])
    g.rpb = din("rpb", [DEPTH, 120, 31])
    g.w_out = din("w_out", [DEPTH, D, D])
    g.norm2_g = din("norm2_g", [DEPTH, D])
    g.w1 = din("w_mlp1", [DEPTH, D, DFF])
    g.w2 = din("w_mlp2", [DEPTH, DFF, D])
    g.final_g = din("final_g", [1, D])
    g.cst = din("cst", [128, 640])
    g.rope = din("rope", [128, 2048])
    g.nam = din("nam", [128, NPAT * 128])
    g.out = nc.dram_tensor("out", [S, D], F32, kind="ExternalOutput").ap()

    g.xs = dscr("xs", [NT, 128, D], F32)
    g.QKT = [dscr("QKT%d" % d, [NT, 128, 1024], BF16) for d in range(2)]
    g.KT = [dscr("KT%d" % d, [NT, 128, 512], BF16) for d in range(2)]
    g.VA = dscr("VA", [NT, 128, 528], BF16)
    g.SO = dscr("SO", [NT, 128, 512], BF16)
    g.QN = dscr("QN", [NT, 128, 512], BF16)
    g.KN = dscr("KN", [NT, 128, 512], BF16)
    g.VN = dscr("VN", [NT, 128, 528], BF16)
    g.H = [dscr("H%d" % d, [NT, 128, 512], F32) for d in range(2)]
    g.Y = dscr("Y", [NT, 128, 1024], BF16)
    g.modD = dscr("modD", [DEPTH, 2, 6 * D], F32)
    g.rpbpad = dscr("rpbpad", [DEPTH, 120, 128], F32)

    ABYTES = 206 * 1024
    g.A = P.sb("A", [128, ABYTES // 2], BF16)
    g.PS = P.ps("PS", [128, 4096], F32)
    g.top = 0
    g.pats, g.nakeys = na_patterns()

    def al(nbytes):
        o = g.top
        g.top = (g.top + nbytes + 63) // 64 * 64
        assert g.top <= ABYTES, ("SBUF overflow", g.top)
        return o

    def V(off, dt, *dims, parts=None):
        n = 1
        for d_ in dims:
            n *= d_
        sz = 4 if dt == F32 else 2
        ap = g.A[:, off // 2:(off + n * sz) // 2]
        if dt == F32:
            ap = ap.bitcast(F32)
        if len(dims) == 2:
            ap = ap.rearrange("p (a b) -> p a b", a=dims[0])
        elif len(dims) == 3:
            ap = ap.rearrange("p (a b c) -> p a b c", a=dims[0], b=dims[1])
        elif len(dims) == 4:
            ap = ap.rearrange("p (a b c d) -> p a b c d", a=dims[0], b=dims[1], c=dims[2])
        return ap

    def T(dt, *dims):
        n = 1
        for d_ in dims:
            n *= d_
        return V(al(n * (4 if dt == F32 else 2)), dt, *dims)

    g.al, g.V, g.T = al, V, T

    def bank(i, dt=F32):
        ap = g.PS[:, i * 512:(i + 1) * 512]
        if dt == BF16:
            ap = ap.bitcast(BF16)
        return ap
    g.bank = bank

    ph = os.environ.get("PHASES", "pabcd")
    prologue(g)
    for l in range(depth):
        layer_setup(g, l)
        if "a" in ph:
            phase_a(g, l)
        if "b" in ph:
            phase_b(g, l)
        if "c" in ph:
            phase_c(g, l)
        if "d" in ph:
            phase_d2(g, l)
    P.finish()
    return nc


def _aps(*xs):
    return [x for x in xs if x is not None and not isinstance(x, (int, float))]


def op_tt(P, eng, out, a, b, op):
    P.op(eng, lambda e: e.tensor_tensor(out, a, b, op), [a, b], [out])


def op_ts(P, eng, out, a, s1, s2, op0, op1=None):
    if op1 is None:
        nm = {ALU.mult: "tensor_scalar_mul", ALU.max: "tensor_scalar_max", ALU.add: "tensor_scalar_add"}[op0]
        P.op(eng, lambda e: getattr(e, nm)(out, a, s1), _aps(a, s1), [out])
    else:
        P.op(eng, lambda e: e.tensor_scalar(out, a, s1, s2, op0, op1), _aps(a, s1, s2), [out])


def op_tsm(P, eng, out, a, s1):
    P.op(eng, lambda e: e.tensor_scalar_mul(out, a, s1), _aps(a, s1), [out])


def op_cp(P, eng, out, a):
    if eng == "act":
        P.op(eng, lambda e: e.copy(out, a), [a], [out])
    else:
        P.op(eng, lambda e: e.tensor_copy(out, a), [a], [out])


def op_act(P, out, a, func, bias=None, scale=1.0, accum_out=None):
    kw = {}
    if bias is not None:
        kw["bias"] = bias
    if accum_out is not None:
        kw["accum_out"] = accum_out
    P.op("act", lambda e: e.activation(out, a, func, scale=scale, **kw), _aps(a, bias, scale),
         [out] + ([accum_out] if accum_out is not None else []))


def op_mm(P, out, lhsT, rhs, start=True, stop=True):
    P.op("pe", lambda e: e.matmul(out, lhsT, rhs, start=start, stop=stop), [lhsT, rhs], [out])


def op_tr(P, out, in_, ident):
    P.op("pe", lambda e: e.transpose(out, in_, ident), [in_, ident], [out])


def op_memset(P, eng, out, val):
    P.op(eng, lambda e: e.memset(out, val), [], [out])


def load_cast(g, dst, src, stg, idx):
    st = stg[idx % len(stg)]
    nd = len(dst.shape)
    if nd == 2:
        v = st[:, 0:dst.shape[1]]
    else:
        v = st[:, 0:dst.shape[1] * dst.shape[2]].rearrange("p (a b) -> p a b", a=dst.shape[1])
    g.P.dma("sp", v, src)
    op_cp(g.P, "dve" if idx % 2 == 0 else "act", dst, v)


def cap(base, dims, off=0):
    return bass.AP(base.tensor, int(base.offset) + off, [list(d) for d in dims])


def bc_mid(ap2, n):
    d = [list(x) for x in ap2.ap]
    return bass.AP(ap2.tensor, int(ap2.offset), [d[0], [0, n], d[1]])


def bc_last(ap2, n):
    d = [list(x) for x in ap2.ap]
    return bass.AP(ap2.tensor, int(ap2.offset), [d[0], d[1], [0, n]])


def prologue(g):
    P, T, bank = g.P, g.T, g.bank
    g.cstf = T(F32, 640)
    g.cb = T(BF16, 512)
    g.Eend = T(F32, NT, 16)
    g.VT = T(F32, 8, 16)
    g.SCL = T(F32, 4, 8)
    g.G1 = T(F32, 1024)
    g.G2 = T(F32, 1024)
    g.ngbc = T(F32, 512)
    g.bgbc = T(F32, 32)
    g.mark = g.top
    P.dma("sp", g.cstf, g.cst)
    op_cp(P, "dve", g.cb, g.cstf[:, 0:512])
    g.identf = g.cstf[:, 0:128]
    g.trifw = g.cstf[:, 256:384]
    g.tribw = g.cstf[:, 384:512]
    g.ones = g.cstf[:, 512:640]
    g.identb = g.cb[:, 0:128]
    g.identJb = g.cb[:, 128:256]
    g.maskb = [g.cb[:, 256:384], g.cb[:, 384:512]]
    ccf = T(F32, 1024)
    cs = T(F32, 1024)
    sT = T(BF16, 8, 2)
    brow = T(F32, 6144)
    modrow = T(F32, 6144)
    wa = [T(BF16, 8, 512) for _ in range(2)]
    wstg = [T(F32, 4096) for _ in range(2)]
    zt = T(F32, 128)
    P.dma("sp", ccf[0:2, :], g.cc)
    op_act(P, cs[0:2, :], ccf[0:2, :], AF.Silu)
    pst = bank(0)
    for k in range(8):
        op_tr(P, pst[:, k * 2:(k + 1) * 2], cs[0:2, k * 128:(k + 1) * 128], g.identf[0:2, 0:2])
    op_cp(P, "dve", sT, pst[:, 0:16].rearrange("p (a b) -> p a b", a=8))
    for l in range(g.depth):
        P.dma("sp", brow[0:2, :], cap(g.b_ada, [[0, 2], [1, 6144]], off=l * 6144))
        for n in range(12):
            w = wa[(l * 12 + n) % 2]
            load_cast(g, w, g.w_ada[l, :, n * 512:(n + 1) * 512].rearrange("(k p) n -> p k n", p=128), wstg, l * 12 + n)
            pm = bank(2 + (n % 2))
            for k in range(8):
                op_mm(P, pm[0:2, :], sT[:, k, :], w[:, k, :], start=(k == 0), stop=(k == 7))
            op_tt(P, "dve", modrow[0:2, n * 512:(n + 1) * 512], pm[0:2, :], brow[0:2, n * 512:(n + 1) * 512], ALU.add)
        P.dma("sp", g.modD[l], modrow[0:2, :])
    op_memset(P, "pool", zt, 0.0)
    for l in range(g.depth):
        P.dma("sp", zt[0:120, 48:79], g.rpb[l])
        P.dma("sp", g.rpbpad[l], zt[0:120, :])


def load_gates(g, l, s):
    g.P.dma("sp", g.G1, cap(g.modD, [[0, 128], [1, 1024]], off=(l * 2 + s) * 6144 + 2048))
    g.P.dma("sp", g.G2, cap(g.modD, [[0, 128], [1, 1024]], off=(l * 2 + s) * 6144 + 5120))


def layer_setup(g, l):
    P, T, bank = g.P, g.T, g.bank
    g.top = g.mark
    Vr = T(F32, 1024)
    op_memset(P, "pool", Vr[0:16, :], 0.0)
    P.dma("sp", Vr[0:6, :], g.modD[l, 0].rearrange("(r c) -> r c", r=6))
    P.dma("sp", Vr[6:12, :], g.modD[l, 1].rearrange("(r c) -> r c", r=6))
    P.dma("sp", Vr[12:13, :], g.norm1_g[l:l + 1, :])
    P.dma("sp", Vr[13:14, :], g.norm2_g[l:l + 1, :])
    P.dma("sp", Vr[14:15, :], g.final_g)
    pst = bank(0)
    for c in range(8):
        op_tr(P, pst[:, c * 16:(c + 1) * 16], Vr[0:16, c * 128:(c + 1) * 128], g.identf[0:16, 0:16])
    op_cp(P, "dve", g.VT, pst[:, 0:128].rearrange("p (a b) -> p a b", a=8))
    for s in range(2):
        for n in range(2):
            sc = g.VT[:, :, s * 6 + 1 + 3 * n]
            ng = g.VT[:, :, 12 + n]
            o = g.SCL[:, s * 2 + n, :]
            P.op("dve", lambda e, o=o, sc=sc, ng=ng: e.scalar_tensor_tensor(o, sc, 1.0, ng, ALU.add, ALU.mult),
                 [sc, ng], [o])
    P.dma("sp", g.ngbc, cap(g.mng, [[0, 128], [1, 512]], off=l * 512))
    P.dma("sp", g.bgbc, cap(g.b_gate, [[0, 128], [1, 32]], off=l * 32))


def src_tile(g, l, t):
    if l == 0:
        return g.ctx[t * 128:(t + 1) * 128, :] if t < 2 else g.x[(t - 2) * 128:(t - 1) * 128, :]
    return g.xs[t]


def rms_to_hT(g, xt, s, n, hT, junk, ss, xn, ptb):
    P = g.P
    op_act(P, junk, xt, AF.Square, accum_out=ss[:, 0:1])
    op_ts(P, "dve", ss[:, 1:2], ss[:, 0:1], 1.0 / D, EPS, ALU.mult, ALU.add)
    op_act(P, ss[:, 3:4], ss[:, 1:2], AF.Ln)
    op_act(P, ss[:, 2:3], ss[:, 3:4], AF.Exp, scale=-0.5)
    op_tsm(P, "dve", xn, xt, ss[:, 2:3])
    for c in range(8):
        op_tr(P, ptb[:, c * 128:(c + 1) * 128], xn[:, c * 128:(c + 1) * 128], g.identb)
    shr = s * 6 + 3 * n
    for c in range(8):
        op_act(P, hT[:, c, :], ptb[:, c * 128:(c + 1) * 128], AF.Identity,
               bias=g.VT[:, c, shr:shr + 1], scale=g.SCL[:, s * 2 + n, c:c + 1])


def phase_a(g, l):
    P, T, bank = g.P, g.T, g.bank
    g.top = g.mark
    last = (l == g.depth - 1) and not g.nolast
    WIN = T(BF16, 8, DIN)
    ROPE = T(F32, 2, 32, 32)
    xt = [T(F32, 1024) for _ in range(2)]
    junk = T(BF16, 1024)
    ss2 = [T(F32, 4) for _ in range(2)]
    xn2 = [T(BF16, 1024) for _ in range(2)]
    hT2 = [T(BF16, 8, 128) for _ in range(2)]
    GT = T(F32, 32)
    E1 = T(F32, 16)
    L1 = T(F32, 16)
    TG = T(F32, 16)
    SCQK = T(F32, 2, 16)
    QKR = T(F32, 16, 64)
    RA = T(F32, 16, 32)
    RB = T(F32, 16, 32)
    RC = T(F32, 16, 32)
    RD = T(F32, 16, 32)
    QS = [T(BF16, 1024) for _ in range(2)]
    QKTs = [[T(BF16, 1024) for _ in range(2)] for _ in range(2)]
    VAs = [T(BF16, 8, 66) for _ in range(2)]
    SOs = [T(BF16, 512) for _ in range(2)]
    QNs = T(BF16, 1024)
    QKNT = [T(BF16, 1024) for _ in range(2)]
    VNs = [T(BF16, 8, 66) for _ in range(2)]
    wstg = [T(F32, DIN) for _ in range(2)]
    for k in range(8):
        load_cast(g, WIN[:, k, :], g.w_in[l, k * 128:(k + 1) * 128, :], wstg, k)
    P.dma("sp", ROPE, g.rope.rearrange("p (a b c) -> p a b c", a=2, b=32))
    for b in range(2):
        op_memset(P, "pool", VAs[b], 1.0)
        op_memset(P, "pool", VNs[b], 1.0)
    P.dma("sp", xt[0], src_tile(g, l, 0))

    def stage1(t):
        b = t % 2
        rms_to_hT(g, xt[b], 1 if t < 2 else 0, 0, hT2[b], junk, ss2[b], xn2[b], bank(0, BF16))

    stage1(0)
    for t in range(NT):
        b = t % 2
        s = 1 if t < 2 else 0
        hT = hT2[b]
        if t + 1 < NT:
            P.dma("sp", xt[1 - b], src_tile(g, l, t + 1))
        pbank = {7: 6, 0: 2, 1: 3, 2: 4, 3: 5, 4: 2, 5: 3, 6: 4}

        def piece(n):
            w = 512 if n < 7 else 32
            pb = bank(pbank[n])[:, 0:w]
            for k in range(8):
                op_mm(P, pb, hT[:, k, :], WIN[:, k, n * 512:n * 512 + w], start=(k == 0), stop=(k == 7))
            return pb

        pg = piece(7)
        op_tt(P, "dve", GT, pg, g.bgbc, ALU.add)
        op_act(P, E1, GT[:, 16:32], AF.Exp, scale=-1.0)
        op_act(P, L1, E1, AF.Ln, bias=1.0)
        CB = bank(6)[:, 64:112]
        P.op("pe", lambda e, CB=CB: e.matmul(CB[:, 0:8], g.trifw, L1[:, 0:8], start=True, stop=True),
             [g.trifw, L1[:, 0:8]], [CB[:, 0:8]])
        P.op("pe", lambda e, CB=CB: e.matmul(CB[:, 8:16], g.tribw, L1[:, 8:16], start=True, stop=True),
             [g.tribw, L1[:, 8:16]], [CB[:, 8:16]])
        P.op("pe", lambda e, CB=CB: e.matmul(CB[:, 16:32], g.ones, L1, start=True, stop=True),
             [g.ones, L1], [CB[:, 16:32]])
        op_act(P, SCQK[:, :, 0:8], CB[:, 0:16].rearrange("p (a b) -> p a b", a=2), AF.Exp,
               scale=-1.0, bias=math.log(0.125))
        op_tt(P, "dve", TG, GT[:, 0:16], CB[:, 0:16], ALU.add)
        op_act(P, SCQK[:, :, 8:16], TG.rearrange("p (a b) -> p a b", a=2), AF.Exp)
        op_act(P, g.Eend[:, t, :], CB[:, 16:32], AF.Exp, scale=-1.0)
        piece(0)
        piece(1)
        qk = g.PS[:, 1024:2048]
        if t >= 2:
            qk3 = qk.rearrange("p (h e) -> p h e", h=16)
            x1, x2 = qk3[:, :, 0:32], qk3[:, :, 32:64]
            cos = bc_mid(ROPE[:, 0, t - 2, :], 16)
            sin = bc_mid(ROPE[:, 1, t - 2, :], 16)
            op_tt(P, "dve", RA, x1, cos, ALU.mult)
            op_tt(P, "dve", RB, x2, sin, ALU.mult)
            op_tt(P, "dve", QKR[:, :, 0:32], RA, RB, ALU.subtract)
            op_tt(P, "dve", RC, x1, sin, ALU.mult)
            op_tt(P, "dve", RD, x2, cos, ALU.mult)
            op_tt(P, "dve", QKR[:, :, 32:64], RC, RD, ALU.add)
        else:
            op_cp(P, "act", QKR.rearrange("p a b -> p (a b)"), qk)
        for d in range(2):
            sc = bc_last(SCQK[:, d, :], 64)
            op_tt(P, "dve", QS[d].rearrange("p (a b) -> p a b", a=16), QKR, sc, ALU.mult)
        pv = piece(2)
        op_cp(P, "act", VAs[b][:, :, 0:64], pv.rearrange("p (a b) -> p a b", a=8))
        P.dma("sp", g.VA[t], VAs[b].rearrange("p a b -> p (a b)"))
        po = piece(3)
        if not (last and t < 2):
            op_act(P, SOs[b], po, AF.Sigmoid)
            P.dma("sp", g.SO[t], SOs[b])
        if t + 1 < NT:
            stage1(t + 1)
        for d in range(2):
            ptb = bank(1 if d == 0 else 7, BF16)
            for j in range(8):
                op_tr(P, ptb[:, j * 128:(j + 1) * 128], QS[d][:, j * 128:(j + 1) * 128], g.identb)
            op_cp(P, "dve" if d == 0 else "act", QKTs[d][b], ptb)
            P.dma("sp", g.QKT[d][t], QKTs[d][b])
            P.dma("sp", g.KT[d][t], QS[d][:, 512:1024])
        piece(4)
        piece(5)
        op_act(P, QNs[:, 0:512], g.PS[:, 1024:1536], AF.Copy, scale=0.125)
        op_cp(P, "dve", QNs[:, 512:1024], g.PS[:, 1536:2048])
        ptb = bank(1, BF16)
        for j in range(8):
            op_tr(P, ptb[:, j * 128:(j + 1) * 128], QNs[:, j * 128:(j + 1) * 128], g.identJb if j < 4 else g.identb)
        op_cp(P, "dve", QKNT[b], ptb)
        P.dma("sp", g.QN[t], QKNT[b][:, 0:512])
        P.dma("sp", g.KN[t], QKNT[b][:, 512:1024])
        pn = piece(6)
        op_cp(P, "act", VNs[b][:, :, 0:64], pn.rearrange("p (a b) -> p a b", a=8))
        P.dma("sp", g.VN[t], VNs[b].rearrange("p a b -> p (a b)"))


def phase_b(g, l):
    P, T, bank = g.P, g.T, g.bank
    g.top = g.mark
    last = (l == g.depth - 1) and not g.nolast
    BT = T(BF16, NPAT, 8, 128)
    KNall = T(BF16, NT, 512)
    VNall = T(BF16, NT, 528)
    NAM = T(F32, NPAT, 128)
    STG = [T(F32, 8, 128) for _ in range(2)]
    P.dma("sp", NAM, g.nam.rearrange("p (a b) -> p a b", a=NPAT))
    for b_ in range(2):
        op_memset(P, "pool", STG[b_], 0.0)
    setup = []

    def mk_pat(pi, dl):
        def f():
            stg = STG[pi % 2]
            for krl in range(2):
                for qrl in range(2):
                    a_ = 2 * dl + krl - qrl + 7
                    if a_ < 0 or a_ > 14:
                        continue
                    src = cap(g.rpbpad, [[1, 64], [15 * 128, 8], [1, 64]], off=(l * 120 + a_) * 128)
                    P.dma("sp", stg[krl * 64:(krl + 1) * 64, :, qrl * 64:(qrl + 1) * 64], src)
            op_tt(P, "dve", BT[:, pi, :, :], stg, bc_mid(NAM[:, pi, :], 8), ALU.add)
        return f

    def mk_kv(t0):
        def f():
            P.dma("sp", KNall[:, t0:t0 + 2, :], g.KN[t0:t0 + 2].rearrange("t p f -> p t f"))
            P.dma("sp", VNall[:, t0:t0 + 2, :], g.VN[t0:t0 + 2].rearrange("t p f -> p t f"))
        return f

    for pi, (jc, dl) in enumerate(g.pats):
        setup.append(mk_pat(pi, dl))
    for t0 in range(0, NT, 2):
        setup.append(mk_kv(t0))
    QK = [[T(BF16, 2, 4, 128) for _ in range(2)] for _ in range(2)]
    KTb = [[T(BF16, 512) for _ in range(2)] for _ in range(2)]
    VAb = [[T(BF16, 8, 66) for _ in range(2)] for _ in range(2)]
    ST = [T(BF16, 8, 128) for _ in range(2)]
    CF = [T(F32, 8, 66) for _ in range(2)]
    C16 = [T(BF16, 8, 66) for _ in range(2)]
    DEN = [T(F32, 8) for _ in range(2)]
    HB = [[T(F32, 8, 64) for _ in range(2)] for _ in range(2)]
    for d in range(2):
        op_memset(P, "pool", CF[d], 0.0)
        op_memset(P, "pool", C16[d], 0.0)
    order = [list(range(NT)), [1, 0] + list(range(NT - 1, 1, -1))]

    def loads(d, i):
        t = order[d][i]
        b = i % 2
        P.dma("sp", QK[d][b].rearrange("p a b c -> p (a b c)"), g.QKT[d][t])
        P.dma("sp", KTb[d][b], g.KT[d][t])
        P.dma("sp", VAb[d][b].rearrange("p a b -> p (a b)"), g.VA[t])

    for d in range(2):
        loads(d, 0)
    PSf = g.PS[:, :]
    pstr = int(PSf.ap[0][0])
    sbase_ = [1024, 2048]
    nb = 0
    dcb = 3072

    def part1(d, i):
        t = order[d][i]
        b = i % 2
        if i + 1 < NT:
            loads(d, i + 1)
        if setup:
            setup.pop(0)()
        qk, kt, va = QK[d][b], KTb[d][b], VAb[d][b]
        need_out = not (last and t < 2)
        sbase = sbase_[d]
        if need_out:
            for h in range(8):
                hp, base = h // 2, (h % 2) * 64
                so = sbase + (h % 2) * 512 + hp * 128
                op_mm(P, PSf[:, so: so + 128], qk[base:base + 64, 1, hp, :], qk[base:base + 64, 0, hp, :])
        for h in range(8):
            hp = h // 2
            o = PSf[:, dcb + (h // 4) * 512 + (h % 4) * 66: dcb + (h // 4) * 512 + (h % 4) * 66 + 66]
            op_mm(P, o, kt[:, hp * 128:(hp + 1) * 128], va[:, h, :])
        if need_out:
            for j in range(2):
                Sv = PSf[:, sbase + j * 512: sbase + (j + 1) * 512].rearrange("p (h t) -> p h t", h=4)
                op_tt(P, "dve", ST[d][:, j::2, :], Sv, bc_mid(g.maskb[d], 4), ALU.mult)
        for half in range(2):
            dcv = PSf[:, dcb + half * 512: dcb + half * 512 + 264].rearrange("p (h e) -> p h e", h=4)
            cf = CF[d][:, half * 4:(half + 1) * 4, :]
            ee = bc_last(g.Eend[:, t, d * 8 + half * 4: d * 8 + half * 4 + 4], 66)
            op_tt(P, "dve", cf, cf, dcv, ALU.add)
            op_tt(P, "dve", cf, cf, ee, ALU.mult)

    def part2(d, i):
        t = order[d][i]
        b = i % 2
        qk, va = QK[d][b], VAb[d][b]
        need_out = not (last and t < 2)
        if need_out:
            for h in range(8):
                hp, base = h // 2, (h % 2) * 64
                oo = nb + (h % 2) * 512 + hp * 66
                o = PSf[:, oo: oo + 65]
                op_mm(P, o, qk[base:base + 64, 0, hp, :], C16[d][base:base + 64, h, 0:65], start=True, stop=False)
                op_mm(P, o, ST[d][:, h, :], va[:, h, 0:65], start=False, stop=True)
        for half in range(2):
            op_cp(P, "dve", C16[d][:, half * 4:(half + 1) * 4, :], CF[d][:, half * 4:(half + 1) * 4, :])
        if need_out:
            for half in range(2):
                nv = PSf[:, nb + half * 512: nb + half * 512 + 264].rearrange("p (h e) -> p h e", h=4)
                dn = DEN[d][:, half * 4:(half + 1) * 4]
                op_act(P, dn, nv[:, :, 64], AF.Abs)
                op_ts(P, "dve", dn, dn, 1.0, None, ALU.max)
                P.op("dve", lambda e, dn=dn: e.reciprocal(dn, dn), [dn], [dn])
                op_tt(P, "dve", HB[d][b][:, half::2, :], nv[:, :, 0:64], bc_last(dn, 64), ALU.mult)
            P.dma("sp", g.H[d][t], HB[d][b].rearrange("p a b -> p (a b)"))

    for i in range(NT):
        part1(0, i)
        part1(1, i)
        part2(0, i)
        part2(1, i)
    while setup:
        setup.pop(0)()


def phase_c(g, l):
    P, T, bank = g.P, g.T, g.bank
    g.top = g.mark
    last = (l == g.depth - 1) and not g.nolast
    BT = T(BF16, NPAT, 8, 128)
    KNall = T(BF16, NT, 512)
    VNall = T(BF16, NT, 528)
    QNb = [T(BF16, 512) for _ in range(2)]
    PTb = [T(BF16, 7, 128) for _ in range(2)]
    RD = T(F32, 8)
    WO = T(BF16, 8, 1024)
    top_c = g.top
    wstg = [T(F32, 1024) for _ in range(2)]
    g.top = top_c
    xt = [T(F32, 1024) for _ in range(2)]
    Yt = [T(BF16, 1024) for _ in range(2)]
    HFb = [T(F32, 8, 64) for _ in range(2)]
    HBb = [T(F32, 8, 64) for _ in range(2)]
    SOb = [T(BF16, 512) for _ in range(2)]
    yT = T(BF16, 8, 128)
    st8 = T(F32, 4, 8)
    TMP = T(F32, 1024)
    HS = T(F32, 8, 64)
    for k in range(8):
        load_cast(g, WO[:, k, :], g.w_out[l, k * 128:(k + 1) * 128, :], wstg, k)
    qtiles = list(range(2, NT)) + ([] if last else [0, 1])
    PSf = g.PS[:, :]

    def loads(qi):
        t_ = qtiles[qi]
        b_ = qi % 2
        P.dma("sp", QNb[b_], g.QN[t_])
        P.dma("sp", xt[b_], src_tile(g, l, t_))
        P.dma("sp", HFb[b_].rearrange("p a b -> p (a b)"), g.H[0][t_])
        P.dma("sp", HBb[b_].rearrange("p a b -> p (a b)"), g.H[1][t_])
        P.dma("sp", SOb[b_], g.SO[t_])

    cur_s = None
    loads(0)
    for qi, t in enumerate(qtiles):
        b = qi % 2
        s = 1 if t < 2 else 0
        if s != cur_s:
            load_gates(g, l, s)
            cur_s = s
        if qi + 1 < len(qtiles):
            loads(qi + 1)
        op_tt(P, "dve", HS, HFb[b], HBb[b], ALU.add)
        P.op("dve", lambda e: e.reduce_sum(st8[:, 0, :], HS, AX.X), [HS], [st8[:, 0, :]])
        op_ts(P, "dve", st8[:, 1, :], st8[:, 0, :], 1.0 / 64, None, ALU.mult)
        op_tt(P, "dve", HS, HS, bc_last(st8[:, 1, :], 64), ALU.subtract)
        TM3 = TMP[:, 0:512].rearrange("p (a b) -> p a b", a=8)
        op_tt(P, "dve", TM3, HS, HS, ALU.mult)
        P.op("dve", lambda e: e.reduce_sum(st8[:, 2, :], TM3, AX.X), [TM3], [st8[:, 2, :]])
        op_ts(P, "dve", st8[:, 3, :], st8[:, 2, :], 1.0 / 64, EPS, ALU.mult, ALU.add)
        op_act(P, st8[:, 2, :], st8[:, 3, :], AF.Ln)
        op_act(P, st8[:, 3, :], st8[:, 2, :], AF.Exp, scale=-0.5)
        op_tt(P, "dve", HS, HS, bc_last(st8[:, 3, :], 64), ALU.mult)
        HS2 = HS.rearrange("p a b -> p (a b)")
        op_tt(P, "dve", HS2, HS2, g.ngbc, ALU.mult)
        op_tt(P, "dve", Yt[b][:, 0:512], HS2, SOb[b], ALU.mult)
        if t >= 2:
            keys = [(tk + 2, pt) for (tk, pt) in g.nakeys[t - 2]] + [(0, None), (1, None)]
        else:
            keys = [(0, None), (1, None)]
        nk = len(keys)
        def emit_qk(h):
            hp, base = h // 2, (h % 2) * 64
            sb = 1024 if h % 2 == 0 else 2048
            for i, (tk, pt) in enumerate(keys):
                o = PSf[:, sb + i * 128: sb + (i + 1) * 128]
                op_mm(P, o, KNall[base:base + 64, tk, hp * 128:(hp + 1) * 128],
                      QNb[b][base:base + 64, hp * 128:(hp + 1) * 128], start=True, stop=(pt is None))
                if pt is not None:
                    op_mm(P, o, g.identb, BT[:, pt, h, :], start=False, stop=True)

        emit_qk(0)
        for h in range(8):
            sb = 1024 if h % 2 == 0 else 2048
            if h + 1 < 8:
                emit_qk(h + 1)
            pt_ = PTb[h % 2]
            op_act(P, pt_[:, 0:nk, :], PSf[:, sb: sb + nk * 128].rearrange("p (a b) -> p a b", a=nk), AF.Exp)
            o = PSf[:, 3072 + (h // 4) * 512 + (h % 4) * 66: 3072 + (h // 4) * 512 + (h % 4) * 66 + 65]
            for i, (tk, pt) in enumerate(keys):
                op_mm(P, o, pt_[:, i, :], VNall[:, tk, h * 66:h * 66 + 65], start=(i == 0), stop=(i == nk - 1))
        for half in range(2):
            nv = PSf[:, 3072 + half * 512: 3072 + half * 512 + 264].rearrange("p (h e) -> p h e", h=4)
            dn = RD[:, half * 4:(half + 1) * 4]
            P.op("dve", lambda e, dn=dn, nv=nv: e.reciprocal(dn, nv[:, :, 64]), [nv[:, :, 64]], [dn])
            yn = Yt[b][:, 512 + half * 256: 512 + (half + 1) * 256].rearrange("p (a b) -> p a b", a=4)
            op_tt(P, "dve", yn, nv[:, :, 0:64], bc_last(dn, 64), ALU.mult)
        x_ = xt[b]
        ptb = bank(0, BF16)
        for c in range(8):
            op_tr(P, ptb[:, c * 128:(c + 1) * 128], Yt[b][:, c * 128:(c + 1) * 128], g.identb if c < 4 else g.identJb)
        op_cp(P, "dve", yT.rearrange("p a b -> p (a b)"), ptb)
        for n in range(2):
            pb = bank(1 - n)
            for k in range(8):
                op_mm(P, pb, yT[:, k, :], WO[:, k, n * 512:(n + 1) * 512], start=(k == 0), stop=(k == 7))
            op_tt(P, "dve", TMP[:, n * 512:(n + 1) * 512], pb, g.G1[:, n * 512:(n + 1) * 512], ALU.mult)
        op_tt(P, "dve", x_, x_, TMP, ALU.add)
        P.dma("sp", g.xs[t], x_)


def phase_d1(g, l):
    P, T, bank = g.P, g.T, g.bank
    g.top = g.mark
    last = (l == g.depth - 1) and not g.nolast
    WO = T(BF16, 8, 1024)
    xt = [T(F32, 1024) for _ in range(2)]
    Yt = [T(BF16, 1024) for _ in range(2)]
    HFb = [T(F32, 8, 64) for _ in range(2)]
    HBb = [T(F32, 8, 64) for _ in range(2)]
    SOb = [T(BF16, 512) for _ in range(2)]
    yT = T(BF16, 8, 128)
    st8 = T(F32, 4, 8)
    TMP = T(F32, 1024)
    HS = T(F32, 8, 64)
    wstg = [T(F32, 1024) for _ in range(2)]
    for k in range(8):
        load_cast(g, WO[:, k, :], g.w_out[l, k * 128:(k + 1) * 128, :], wstg, k)
    tiles = list(range(NT)) if not last else list(range(2, NT))
    PSf = g.PS[:, :]

    def loads(i):
        t = tiles[i]
        b = i % 2
        P.dma("sp", xt[b], src_tile(g, l, t))
        P.dma("sp", HFb[b].rearrange("p a b -> p (a b)"), g.H[0][t])
        P.dma("sp", HBb[b].rearrange("p a b -> p (a b)"), g.H[1][t])
        P.dma("sp", SOb[b], g.SO[t])
        P.dma("sp", Yt[b][:, 512:1024], g.Y[t, :, 512:1024])

    cur_s = None
    loads(0)
    for i, t in enumerate(tiles):
        b = i % 2
        s = 1 if t < 2 else 0
        if s != cur_s:
            load_gates(g, l, s)
            cur_s = s
        if i + 1 < len(tiles):
            loads(i + 1)
        x_ = xt[b]
        op_tt(P, "dve", HS, HFb[b], HBb[b], ALU.add)
        P.op("dve", lambda e: e.reduce_sum(st8[:, 0, :], HS, AX.X), [HS], [st8[:, 0, :]])
        op_ts(P, "dve", st8[:, 1, :], st8[:, 0, :], 1.0 / 64, None, ALU.mult)
        op_tt(P, "dve", HS, HS, bc_last(st8[:, 1, :], 64), ALU.subtract)
        TM3 = TMP[:, 0:512].rearrange("p (a b) -> p a b", a=8)
        op_tt(P, "dve", TM3, HS, HS, ALU.mult)
        P.op("dve", lambda e: e.reduce_sum(st8[:, 2, :], TM3, AX.X), [TM3], [st8[:, 2, :]])
        op_ts(P, "dve", st8[:, 3, :], st8[:, 2, :], 1.0 / 64, EPS, ALU.mult, ALU.add)
        op_act(P, st8[:, 2, :], st8[:, 3, :], AF.Ln)
        op_act(P, st8[:, 3, :], st8[:, 2, :], AF.Exp, scale=-0.5)
        op_tt(P, "dve", HS, HS, bc_last(st8[:, 3, :], 64), ALU.mult)
        HS2 = HS.rearrange("p a b -> p (a b)")
        op_tt(P, "dve", HS2, HS2, g.ngbc, ALU.mult)
        op_tt(P, "dve", Yt[b][:, 0:512], HS2, SOb[b], ALU.mult)
        ptb = bank(0, BF16)
        for c in range(8):
            op_tr(P, ptb[:, c * 128:(c + 1) * 128], Yt[b][:, c * 128:(c + 1) * 128], g.identb if c < 4 else g.identJb)
        op_cp(P, "dve", yT.rearrange("p a b -> p (a b)"), ptb)
        for n in range(2):
            pb = bank(2 + n)
            for k in range(8):
                op_mm(P, pb, yT[:, k, :], WO[:, k, n * 512:(n + 1) * 512], start=(k == 0), stop=(k == 7))
            op_tt(P, "dve", TMP[:, n * 512:(n + 1) * 512], pb, g.G1[:, n * 512:(n + 1) * 512], ALU.mult)
        op_tt(P, "dve", x_, x_, TMP, ALU.add)
        P.dma("sp", g.xs[t], x_)


def phase_d2(g, l):
    P, T, bank = g.P, g.T, g.bank
    g.top = g.mark
    last = (l == g.depth - 1) and not g.nolast
    W1 = T(BF16, 8, DFF)
    W2 = T(BF16, 32, 1024)
    xt = [T(F32, 1024) for _ in range(2)]
    junk = T(BF16, 1024)
    ss2 = [T(F32, 4) for _ in range(2)]
    xn2 = [T(BF16, 1024) for _ in range(2)]
    hT2 = [T(BF16, 8, 128) for _ in range(2)]
    ss = T(F32, 4)
    top0 = g.top
    wstg = [T(F32, 2048) for _ in range(2)]
    g.top = top0
    U = T(BF16, DFF)
    uT = T(BF16, 32, 128)
    R = [T(BF16, 512) for _ in range(2)]
    TMP = T(F32, 1024)
    if last:
        g.FG = T(F32, 1024)
        P.dma("sp", g.FG, cap(g.final_g, [[0, 128], [1, 1024]]))
    for k in range(16):
        load_cast(g, W1[:, k // 2, (k % 2) * 2048:(k % 2 + 1) * 2048],
                  g.w1[l, (k // 2) * 128:(k // 2 + 1) * 128, (k % 2) * 2048:(k % 2 + 1) * 2048], wstg, k)
    for k in range(16):
        load_cast(g, W2[:, k * 2:(k + 1) * 2, :],
                  g.w2[l, k * 256:(k + 1) * 256, :].rearrange("(k p) n -> p k n", p=128), wstg, k)
    tiles = list(range(NT)) if not last else list(range(2, NT))
    cur_s = None
    P.dma("sp", xt[0], g.xs[tiles[0]])

    def stage1(i):
        t_ = tiles[i]
        rms_to_hT(g, xt[i % 2], 1 if t_ < 2 else 0, 1, hT2[i % 2], junk, ss2[i % 2], xn2[i % 2], bank(1, BF16))

    stage1(0)
    for i, t in enumerate(tiles):
        b = i % 2
        s = 1 if t < 2 else 0
        if s != cur_s:
            load_gates(g, l, s)
            cur_s = s
        if i + 1 < len(tiles):
            P.dma("sp", xt[1 - b], g.xs[tiles[i + 1]])
        x_ = xt[b]
        hT = hT2[b]
        for m in range(8):
            pb = bank(4 + (m % 4))
            for k in range(8):
                op_mm(P, pb, hT[:, k, :], W1[:, k, m * 512:(m + 1) * 512], start=(k == 0), stop=(k == 7))
            r = R[m % 2]
            op_act(P, r, pb, AF.Relu)
            op_tt(P, "dve", U[:, m * 512:(m + 1) * 512], r, r, ALU.mult)
        if i + 1 < len(tiles):
            stage1(i + 1)
        for q in range(4):
            ptb = bank(q % 2, BF16)
            for c in range(8):
                cc_ = q * 8 + c
                op_tr(P, ptb[:, c * 128:(c + 1) * 128], U[:, cc_ * 128:(cc_ + 1) * 128], g.identb)
            op_cp(P, "dve", uT[:, q * 8:(q + 1) * 8, :].rearrange("p a b -> p (a b)"), ptb)
        for n in range(2):
            pb = bank(2 + n)
            for m in range(32):
                op_mm(P, pb, uT[:, m, :], W2[:, m, n * 512:(n + 1) * 512], start=(m == 0), stop=(m == 31))
            op_tt(P, "dve", TMP[:, n * 512:(n + 1) * 512], pb, g.G2[:, n * 512:(n + 1) * 512], ALU.mult)
        op_tt(P, "dve", x_, x_, TMP, ALU.add)
        if not last:
            P.dma("sp", g.xs[t], x_)
        else:
            op_act(P, junk, x_, AF.Square, accum_out=ss[:, 0:1])
            op_ts(P, "dve", ss[:, 1:2], ss[:, 0:1], 1.0 / D, EPS, ALU.mult, ALU.add)
            op_act(P, ss[:, 3:4], ss[:, 1:2], AF.Ln)
            op_act(P, ss[:, 2:3], ss[:, 3:4], AF.Exp, scale=-0.5)
            op_tsm(P, "dve", TMP, x_, ss[:, 2:3])
            op_tt(P, "dve", TMP, TMP, g.FG, ALU.mult)
            P.dma("sp", g.out[(t - 2) * 128:(t - 1) * 128, :], TMP)


_CACHE = {}


def make_in_maps(inputs):
    cst, rope, nam = host_constants()
    f = lambda a: np.ascontiguousarray(np.asarray(a, dtype=np.float32))
    shared = {
        "w_ada": f(inputs["w_ada"]), "b_ada": f(inputs["b_ada"]), "norm1_g": f(inputs["norm1_g"]),
        "w_in": f(inputs["w_in"]), "b_gate": f(inputs["b_gate"]), "mlstm_norm_g": f(inputs["mlstm_norm_g"]),
        "rpb": f(inputs["rpb"]).reshape(DEPTH, 120, 31), "w_out": f(inputs["w_out"]),
        "norm2_g": f(inputs["norm2_g"]), "w_mlp1": f(inputs["w_mlp1"]), "w_mlp2": f(inputs["w_mlp2"]),
        "final_g": f(inputs["final_g"]).reshape(1, D), "cst": cst, "rope": rope, "nam": nam,
    }
    x = f(inputs["x"])
    c = f(inputs["c"])
    ctx = f(inputs["ctx"])
    c_ctx = f(inputs["c_ctx"])
    maps = []
    for b in range(8):
        m = dict(shared)
        m["x"] = x[b]
        m["ctx"] = ctx[b]
        m["cc"] = np.ascontiguousarray(np.stack([c[b], c_ctx], axis=0))
        maps.append(m)
    return maps


def kernel(**inputs):
    if "nc" not in _CACHE:
        _CACHE["nc"] = build_program()
    nc = _CACHE["nc"]
    maps = make_in_maps(inputs)
    res = run_bass_kernel_spmd(nc, maps, core_ids=list(range(8)))
    out = np.stack([np.asarray(res.results[b]["out"], dtype=np.float32) for b in range(8)], axis=0)
    return out
```

```python
import os
import numpy as np
from contextlib import ExitStack
import concourse.bass as bass
import concourse.mybir as mybir
from concourse.bass_utils import run_bass_kernel_spmd

F32 = mybir.dt.float32
BF16 = mybir.dt.bfloat16
AF = mybir.ActivationFunctionType
ALU = mybir.AluOpType
AX = mybir.AxisListType


class Prog:
    ENGS = ("pe", "dve", "act", "pool", "sp")
    EPOCH = 12000
    NSLOT = 10

    def __init__(self, nc):
        self.nc = nc
        self.stack = ExitStack()
        self.ops = []
        self.hist = {}
        self.flag = []

    def sb(self, name, shape, dtype):
        return self.stack.enter_context(self.nc.sbuf_tensor(name, list(shape), dtype))

    def ps(self, name, shape, dtype):
        return self.stack.enter_context(self.nc.psum_tensor(name, list(shape), dtype))

    @staticmethod
    def region(ap):
        t = ap.tensor
        name = t.name
        sz = 4 if ap.dtype == F32 else 2
        dims = [(int(s) * sz, int(c)) for s, c in ap.ap]
        off = int(ap.offset) * sz
        if "DRam" in type(t).__name__:
            lo = off + sum(min(0, s * (c - 1)) for s, c in dims)
            hi = off + sum(max(0, s * (c - 1)) for s, c in dims) + sz
            return (name, 0, 1, lo, hi)
        pstride = sz
        for d in t.shape[1:]:
            pstride *= int(d)
        ps, pc = dims[0]
        if ps == 0:
            pc = 1
        p0 = off // pstride
        base = off % pstride
        rest = dims[1:]
        lo = base + sum(min(0, s * (c - 1)) for s, c in rest)
        hi = base + sum(max(0, s * (c - 1)) for s, c in rest) + sz
        p1 = p0 + pc
        if "PSum" in type(t).__name__:
            lo = lo // 2048 * 2048
            hi = (hi + 2047) // 2048 * 2048
            p0, p1 = 0, 128
        return (name, p0, p1, lo, hi)

    def _deps_and_update(self, oid, eng, is_dma, reads, writes):
        deps = set()
        rregs = [self.region(a) if not isinstance(a, tuple) else a for a in reads]
        wregs = [self.region(a) if not isinstance(a, tuple) else a for a in writes]
        for (name, p0, p1, lo, hi) in rregs:
            recs = self.hist.get(name)
            if recs is None:
                recs = self.hist[name] = []
            hit = False
            for r in recs:
                if r[0] < p1 and p0 < r[1] and r[2] < hi and lo < r[3]:
                    if r[4] is not None:
                        deps.add(r[4])
                    if r[0] <= p0 and p1 <= r[1] and r[2] <= lo and hi <= r[3]:
                        hit = True
                    self._add_reader(r, oid, eng, is_dma)
            if not hit:
                r = [p0, p1, lo, hi, None, {}, []]
                self._add_reader(r, oid, eng, is_dma)
                recs.append(r)
        for (name, p0, p1, lo, hi) in wregs:
            recs = self.hist.get(name)
            if recs is None:
                recs = self.hist[name] = []
            keep = []
            for r in recs:
                if r[0] < p1 and p0 < r[1] and r[2] < hi and lo < r[3]:
                    if r[4] is not None:
                        deps.add(r[4])
                    deps.update(r[5].values())
                    deps.update(r[6])
                    if p0 <= r[0] and r[1] <= p1 and lo <= r[2] and r[3] <= hi:
                        continue
                keep.append(r)
            keep.append([p0, p1, lo, hi, oid, {}, []])
            self.hist[name] = keep
        deps.discard(oid)
        return deps

    @staticmethod
    def _add_reader(r, oid, eng, is_dma):
        if is_dma:
            r[6].append(oid)
            if len(r[6]) > 64:
                del r[6][0:len(r[6]) - 64]
        else:
            r[5][eng] = oid

    def op(self, eng, fn, reads, writes, dma=False):
        oid = len(self.ops)
        extra = [a for a in reads if not isinstance(a, tuple) and "PSum" in type(a.tensor).__name__]
        if extra:
            writes = list(writes) + extra
        deps = self._deps_and_update(oid, eng, dma, reads, writes)
        self.ops.append((eng, fn, deps, dma))
        return oid

    def dma(self, q, out, in_, **kw):
        return self.op(q, lambda e: e.dma_start(out=out, in_=in_, **kw), [in_], [out], dma=True)

    def finish(self):
        nc = self.nc
        ops = self.ops
        n = len(ops)
        flagged = [False] * n
        for (eng, fn, deps, dma) in ops:
            for d in deps:
                de = ops[d][0]
                if ops[d][3]:
                    continue
                if de == "pe" and eng == "pe" and not dma:
                    continue
                flagged[d] = True
        token = [None] * n
        cnt = {e: 0 for e in self.ENGS}
        dcnt = {e: 0 for e in self.ENGS}
        for i, (eng, fn, deps, dma) in enumerate(ops):
            if dma:
                k = dcnt[eng]
                dcnt[eng] += 1
                token[i] = (("d", eng, k % self.NSLOT), 16 * (k // self.NSLOT + 1))
            elif flagged[i]:
                k = cnt[eng]
                cnt[eng] += 1
                token[i] = (("c", eng, k // self.EPOCH), k % self.EPOCH + 1)
        sems = {}
        for tk in token:
            if tk is not None and tk[0] not in sems:
                sems[tk[0]] = None
        for key in list(sems.keys()):
            sems[key] = self.stack.enter_context(nc.semaphore("s_" + "_".join(str(x) for x in key)))
        per_eng = {e: [] for e in self.ENGS}
        for i, o in enumerate(ops):
            per_eng[o[0]].append(i)
        self.n_wait = 0
        if os.environ.get("KSTATS"):
            print("KSTATS ops", {e: len(v) for e, v in per_eng.items()}, "flagged", cnt, "dmas", dcnt, "sems", len(sems), flush=True)

        def emit(engname, e):
            seen = {}
            dslot_last = {}
            for i in per_eng[engname]:
                (eng, fn, deps, dma) = ops[i]
                need = {}
                for d in deps:
                    if (not ops[d][3]) and ops[d][0] == "pe" and eng == "pe" and not dma:
                        continue
                    key, val = token[d]
                    if seen.get(key, 0) >= val:
                        continue
                    if need.get(key, 0) < val:
                        need[key] = val
                if dma:
                    key, val = token[i]
                    if val > 16 and seen.get(key, 0) < val - 16:
                        if need.get(key, 0) < val - 16:
                            need[key] = val - 16
                for key, val in need.items():
                    e.wait_ge(sems[key], val)
                    seen[key] = val
                    self.n_wait += 1
                ins = fn(e)
                if dma:
                    key, val = token[i]
                    ins.then_inc(sems[key], 16)
                    dslot_last[key] = val
                elif token[i] is not None:
                    ins.then_inc(sems[token[i][0]], 1)
            return seen, dslot_last

        with nc.Block() as block:
            @block.tensor
            def _(e):
                emit("pe", e)

            @block.vector
            def _(e):
                emit("dve", e)

            @block.scalar
            def _(e):
                seen, last = emit("act", e)
                for key, val in last.items():
                    if seen.get(key, 0) < val:
                        e.wait_ge(sems[key], val)

            @block.gpsimd
            def _(e):
                seen, last = emit("pool", e)
                for key, val in last.items():
                    if seen.get(key, 0) < val:
                        e.wait_ge(sems[key], val)

            @block.sync
            def _(e):
                seen, last = emit("sp", e)
                for key, val in last.items():
                    if seen.get(key, 0) < val:
                        e.wait_ge(sems[key], val)
        self.stack.close()


import math

D = 1024
S = 4096
CTX = 256
DEPTH = 4
NT = 34
DIN = 3616
DFF = 4096
EPS = 1e-6
NPAT = 21
NEG = -30000.0


def na_patterns():
    pats = []
    keys = {}

    def pid(jc, dl):
        k = (jc, dl)
        if k not in pats:
            pats.append(k)
        return pats.index(k)

    for j in range(32):
        if j == 0:
            jc, dls = "b0", [0, 1, 2, 3]
        elif j == 1:
            jc, dls = "b1", [-1, 0, 1, 2]
        elif j == 30:
            jc, dls = "b30", [-2, -1, 0, 1]
        elif j == 31:
            jc, dls = "b31", [-3, -2, -1, 0]
        else:
            jc, dls = "in", [-2, -1, 0, 1, 2]
        keys[j] = [(j + dl, pid(jc, dl)) for dl in dls]
    return pats, keys


def host_constants():
    cst = np.zeros((128, 640), np.float32)
    cst[:, 0:128] = np.eye(128, dtype=np.float32)
    J = np.zeros((128, 128), np.float32)
    for p in range(128):
        J[p, (p // 64) * 64 + 63 - (p % 64)] = 1.0
    cst[:, 128:256] = J
    s_ = np.arange(128)[:, None]
    t_ = np.arange(128)[None, :]
    cst[:, 256:384] = (s_ <= t_).astype(np.float32)
    cst[:, 384:512] = (s_ >= t_).astype(np.float32)
    cst[:, 512:640] = 1.0
    t = np.arange(S)
    row = (t // 64).astype(np.float32)
    col = (t % 64).astype(np.float32)
    n_freq = 16
    inv_freq = (np.float32(10000.0) ** (-np.arange(n_freq, dtype=np.float32) / np.float32(n_freq))).astype(np.float32)
    ang = np.concatenate([row[:, None] * inv_freq, col[:, None] * inv_freq], axis=-1).astype(np.float32)
    cos = np.cos(ang).astype(np.float32).reshape(32, 128, 32).transpose(1, 0, 2)
    sin = np.sin(ang).astype(np.float32).reshape(32, 128, 32).transpose(1, 0, 2)
    rope = np.ascontiguousarray(np.stack([cos, sin], axis=1)).reshape(128, 2 * 32 * 32)
    pats, keys = na_patterns()
    nam = np.full((128, NPAT, 128), NEG, np.float32)
    jrep = {"b0": 0, "b1": 1, "b30": 30, "b31": 31, "in": 10}
    for pi, (jc, dl) in enumerate(pats):
        j = jrep[jc]
        for kp in range(128):
            kr = 2 * (j + dl) + kp // 64
            kc = kp % 64
            for u in range(128):
                qr = 2 * j + u // 64
                qc = 63 - (u % 64)
                rs = min(max(qr - 4, 0), 56)
                cs = min(max(qc - 8, 0), 48)
                if rs <= kr < rs + 8 and cs <= kc < cs + 16:
                    nam[kp, pi, u] = 0.0
    return cst, rope, nam.reshape(128, NPAT * 128)


class Ctx:
    pass


def build_program(depth=DEPTH, dbg=()):
    nc = bass.Bass("TRN2", target_bir_lowering=False)
    P = Prog(nc)
    g = Ctx()
    g.nc, g.P, g.depth = nc, P, depth
    g.nolast = bool(os.environ.get("NOLAST"))

    def din(name, shape, dt=F32):
        return nc.dram_tensor(name, list(shape), dt, kind="ExternalInput").ap()

    def dscr(name, shape, dt):
        if name in dbg:
            return nc.dram_tensor(name, list(shape), dt, kind="ExternalOutput").ap()
        return nc.dram_tensor(name, list(shape), dt).ap()

    g.x = din("x", [S, D])
    g.ctx = din("ctx", [CTX, D])
    g.cc = din("cc", [2, D])
    g.w_ada = din("w_ada", [DEPTH, D, 6 * D])
    g.b_ada = din("b_ada", [DEPTH, 6 * D])
    g.norm1_g = din("norm1_g", [DEPTH, D])
    g.w_in = din("w_in", [DEPTH, D, DIN])
    g.b_gate = din("b_gate", [DEPTH, 32])
    g.mng = din("mlstm_norm_g", [DEPTH, 512])
    g.rpb = din("rpb", [DEPTH, 120, 31])
    g.w_out = din("w_out", [DEPTH, D, D])
    g.norm2_g = din("norm2_g", [DEPTH, D])
    g.w1 = din("w_mlp1", [DEPTH, D, DFF])
    g.w2 = din("w_mlp2", [DEPTH, DFF, D])
    g.final_g = din("final_g", [1, D])
    g.cst = din("cst", [128, 640])
    g.rope = din("rope", [128, 2048])
    g.nam = din("nam", [128, NPAT * 128])
    g.out = nc.dram_tensor("out", [S, D], F32, kind="ExternalOutput").ap()

    g.xs = dscr("xs", [NT, 128, D], F32)
    g.QKT = [dscr("QKT%d" % d, [NT, 128, 1024], BF16) for d in range(2)]
    g.KT = [dscr("KT%d" % d, [NT, 128, 512], BF16) for d in range(2)]
    g.VA = dscr("VA", [NT, 128, 528], BF16)
    g.SO = dscr("SO", [NT, 128, 512], BF16)
    g.QN = dscr("QN", [NT, 128, 512], BF16)
    g.KN = dscr("KN", [NT, 128, 512], BF16)
    g.VN = dscr("VN", [NT, 128, 528], BF16)
    g.H = [dscr("H%d" % d, [NT, 128, 512], F32) for d in range(2)]
    g.Y = dscr("Y", [NT, 128, 1024], BF16)
    g.modD = dscr("modD", [DEPTH, 2, 6 * D], F32)
    g.rpbpad = dscr("rpbpad", [DEPTH, 120, 128], F32)

    ABYTES = 206 * 1024
    g.A = P.sb("A", [128, ABYTES // 2], BF16)
    g.PS = P.ps("PS", [128, 4096], F32)
    g.top = 0
    g.pats, g.nakeys = na_patterns()

    def al(nbytes):
        o = g.top
        g.top = (g.top + nbytes + 63) // 64 * 64
        assert g.top <= ABYTES, ("SBUF overflow", g.top)
        return o

    def V(off, dt, *dims, parts=None):
        n = 1
        for d_ in dims:
            n *= d_
        sz = 4 if dt == F32 else 2
        ap = g.A[:, off // 2:(off + n * sz) // 2]
        if dt == F32:
            ap = ap.bitcast(F32)
        if len(dims) == 2:
            ap = ap.rearrange("p (a b) -> p a b", a=dims[0])
        elif len(dims) == 3:
            ap = ap.rearrange("p (a b c) -> p a b c", a=dims[0], b=dims[1])
        elif len(dims) == 4:
            ap = ap.rearrange("p (a b c d) -> p a b c d", a=dims[0], b=dims[1], c=dims[2])
        return ap

    def T(dt, *dims):
        n = 1
        for d_ in dims:
            n *= d_
        return V(al(n * (4 if dt == F32 else 2)), dt, *dims)

    g.al, g.V, g.T = al, V, T

    def bank(i, dt=F32):
        ap = g.PS[:, i * 512:(i + 1) * 512]
        if dt == BF16:
            ap = ap.bitcast(BF16)
        return ap
    g.bank = bank

    ph = os.environ.get("PHASES", "pabcd")
    prologue(g)
    for l in range(depth):
        layer_setup(g, l)
        if "a" in ph:
            phase_a(g, l)
        if "b" in ph:
            phase_b(g, l)
        if "c" in ph:
            phase_c(g, l)
        if "d" in ph:
            phase_d2(g, l)
    P.finish()
    return nc


def _aps(*xs):
    return [x for x in xs if x is not None and not isinstance(x, (int, float))]


def op_tt(P, eng, out, a, b, op):
    P.op(eng, lambda e: e.tensor_tensor(out, a, b, op), [a, b], [out])


def op_ts(P, eng, out, a, s1, s2, op0, op1=None):
    if op1 is None:
        nm = {ALU.mult: "tensor_scalar_mul", ALU.max: "tensor_scalar_max", ALU.add: "tensor_scalar_add"}[op0]
        P.op(eng, lambda e: getattr(e, nm)(out, a, s1), _aps(a, s1), [out])
    else:
        P.op(eng, lambda e: e.tensor_scalar(out, a, s1, s2, op0, op1), _aps(a, s1, s2), [out])


def op_tsm(P, eng, out, a, s1):
    P.op(eng, lambda e: e.tensor_scalar_mul(out, a, s1), _aps(a, s1), [out])


def op_cp(P, eng, out, a):
    if eng == "act":
        P.op(eng, lambda e: e.copy(out, a), [a], [out])
    else:
        P.op(eng, lambda e: e.tensor_copy(out, a), [a], [out])


def op_act(P, out, a, func, bias=None, scale=1.0, accum_out=None):
    kw = {}
    if bias is not None:
        kw["bias"] = bias
    if accum_out is not None:
        kw["accum_out"] = accum_out
    P.op("act", lambda e: e.activation(out, a, func, scale=scale, **kw), _aps(a, bias, scale),
         [out] + ([accum_out] if accum_out is not None else []))


def op_mm(P, out, lhsT, rhs, start=True, stop=True):
    P.op("pe", lambda e: e.matmul(out, lhsT, rhs, start=start, stop=stop), [lhsT, rhs], [out])


def op_tr(P, out, in_, ident):
    P.op("pe", lambda e: e.transpose(out, in_, ident), [in_, ident], [out])


def op_memset(P, eng, out, val):
    P.op(eng, lambda e: e.memset(out, val), [], [out])


def load_cast(g, dst, src, stg, idx):
    st = stg[idx % len(stg)]
    nd = len(dst.shape)
    if nd == 2:
        v = st[:, 0:dst.shape[1]]
    else:
        v = st[:, 0:dst.shape[1] * dst.shape[2]].rearrange("p (a b) -> p a b", a=dst.shape[1])
    g.P.dma("sp", v, src)
    op_cp(g.P, "dve" if idx % 2 == 0 else "act", dst, v)


def cap(base, dims, off=0):
    return bass.AP(base.tensor, int(base.offset) + off, [list(d) for d in dims])


def bc_mid(ap2, n):
    d = [list(x) for x in ap2.ap]
    return bass.AP(ap2.tensor, int(ap2.offset), [d[0], [0, n], d[1]])


def bc_last(ap2, n):
    d = [list(x) for x in ap2.ap]
    return bass.AP(ap2.tensor, int(ap2.offset), [d[0], d[1], [0, n]])


def prologue(g):
    P, T, bank = g.P, g.T, g.bank
    g.cstf = T(F32, 640)
    g.cb = T(BF16, 512)
    g.Eend = T(F32, NT, 16)
    g.VT = T(F32, 8, 16)
    g.SCL = T(F32, 4, 8)
    g.G1 = T(F32, 1024)
    g.G2 = T(F32, 1024)
    g.ngbc = T(F32, 512)
    g.bgbc = T(F32, 32)
    g.mark = g.top
    P.dma("sp", g.cstf, g.cst)
    op_cp(P, "dve", g.cb, g.cstf[:, 0:512])
    g.identf = g.cstf[:, 0:128]
    g.trifw = g.cstf[:, 256:384]
    g.tribw = g.cstf[:, 384:512]
    g.ones = g.cstf[:, 512:640]
    g.identb = g.cb[:, 0:128]
    g.identJb = g.cb[:, 128:256]
    g.maskb = [g.cb[:, 256:384], g.cb[:, 384:512]]
    ccf = T(F32, 1024)
    cs = T(F32, 1024)
    sT = T(BF16, 8, 2)
    brow = T(F32, 6144)
    modrow = T(F32, 6144)
    wa = [T(BF16, 8, 512) for _ in range(2)]
    wstg = [T(F32, 4096) for _ in range(2)]
    zt = T(F32, 128)
    P.dma("sp", ccf[0:2, :], g.cc)
    op_act(P, cs[0:2, :], ccf[0:2, :], AF.Silu)
    pst = bank(0)
    for k in range(8):
        op_tr(P, pst[:, k * 2:(k + 1) * 2], cs[0:2, k * 128:(k + 1) * 128], g.identf[0:2, 0:2])
    op_cp(P, "dve", sT, pst[:, 0:16].rearrange("p (a b) -> p a b", a=8))
    for l in range(g.depth):
        P.dma("sp", brow[0:2, :], cap(g.b_ada, [[0, 2], [1, 6144]], off=l * 6144))
        for n in range(12):
            w = wa[(l * 12 + n) % 2]
            load_cast(g, w, g.w_ada[l, :, n * 512:(n + 1) * 512].rearrange("(k p) n -> p k n", p=128), wstg, l * 12 + n)
            pm = bank(2 + (n % 2))
            for k in range(8):
                op_mm(P, pm[0:2, :], sT[:, k, :], w[:, k, :], start=(k == 0), stop=(k == 7))
            op_tt(P, "dve", modrow[0:2, n * 512:(n + 1) * 512], pm[0:2, :], brow[0:2, n * 512:(n + 1) * 512], ALU.add)
        P.dma("sp", g.modD[l], modrow[0:2, :])
    op_memset(P, "pool", zt, 0.0)
    for l in range(g.depth):
        P.dma("sp", zt[0:120, 48:79], g.rpb[l])
        P.dma("sp", g.rpbpad[l], zt[0:120, :])


def load_gates(g, l, s):
    g.P.dma("sp", g.G1, cap(g.modD, [[0, 128], [1, 1024]], off=(l * 2 + s) * 6144 + 2048))
    g.P.dma("sp", g.G2, cap(g.modD, [[0, 128], [1, 1024]], off=(l * 2 + s) * 6144 + 5120))


def layer_setup(g, l):
    P, T, bank = g.P, g.T, g.bank
    g.top = g.mark
    Vr = T(F32, 1024)
    op_memset(P, "pool", Vr[0:16, :], 0.0)
    P.dma("sp", Vr[0:6, :], g.modD[l, 0].rearrange("(r c) -> r c", r=6))
    P.dma("sp", Vr[6:12, :], g.modD[l, 1].rearrange("(r c) -> r c", r=6))
    P.dma("sp", Vr[12:13, :], g.norm1_g[l:l + 1, :])
    P.dma("sp", Vr[13:14, :], g.norm2_g[l:l + 1, :])
    P.dma("sp", Vr[14:15, :], g.final_g)
    pst = bank(0)
    for c in range(8):
        op_tr(P, pst[:, c * 16:(c + 1) * 16], Vr[0:16, c * 128:(c + 1) * 128], g.identf[0:16, 0:16])
    op_cp(P, "dve", g.VT, pst[:, 0:128].rearrange("p (a b) -> p a b", a=8))
    for s in range(2):
        for n in range(2):
            sc = g.VT[:, :, s * 6 + 1 + 3 * n]
            ng = g.VT[:, :, 12 + n]
            o = g.SCL[:, s * 2 + n, :]
            P.op("dve", lambda e, o=o, sc=sc, ng=ng: e.scalar_tensor_tensor(o, sc, 1.0, ng, ALU.add, ALU.mult),
                 [sc, ng], [o])
    P.dma("sp", g.ngbc, cap(g.mng, [[0, 128], [1, 512]], off=l * 512))
    P.dma("sp", g.bgbc, cap(g.b_gate, [[0, 128], [1, 32]], off=l * 32))


def src_tile(g, l, t):
    if l == 0:
        return g.ctx[t * 128:(t + 1) * 128, :] if t < 2 else g.x[(t - 2) * 128:(t - 1) * 128, :]
    return g.xs[t]


def rms_to_hT(g, xt, s, n, hT, junk, ss, xn, ptb):
    P = g.P
    op_act(P, junk, xt, AF.Square, accum_out=ss[:, 0:1])
    op_ts(P, "dve", ss[:, 1:2], ss[:, 0:1], 1.0 / D, EPS, ALU.mult, ALU.add)
    op_act(P, ss[:, 3:4], ss[:, 1:2], AF.Ln)
    op_act(P, ss[:, 2:3], ss[:, 3:4], AF.Exp, scale=-0.5)
    op_tsm(P, "dve", xn, xt, ss[:, 2:3])
    for c in range(8):
        op_tr(P, ptb[:, c * 128:(c + 1) * 128], xn[:, c * 128:(c + 1) * 128], g.identb)
    shr = s * 6 + 3 * n
    for c in range(8):
        op_act(P, hT[:, c, :], ptb[:, c * 128:(c + 1) * 128], AF.Identity,
               bias=g.VT[:, c, shr:shr + 1], scale=g.SCL[:, s * 2 + n, c:c + 1])


def phase_a(g, l):
    P, T, bank = g.P, g.T, g.bank
    g.top = g.mark
    last = (l == g.depth - 1) and not g.nolast
    WIN = T(BF16, 8, DIN)
    ROPE = T(F32, 2, 32, 32)
    xt = [T(F32, 1024) for _ in range(2)]
    junk = T(BF16, 1024)
    ss2 = [T(F32, 4) for _ in range(2)]
    xn2 = [T(BF16, 1024) for _ in range(2)]
    hT2 = [T(BF16, 8, 128) for _ in range(2)]
    GT = T(F32, 32)
    E1 = T(F32, 16)
    L1 = T(F32, 16)
    TG = T(F32, 16)
    SCQK = T(F32, 2, 16)
    QKR = T(F32, 16, 64)
    RA = T(F32, 16, 32)
    RB = T(F32, 16, 32)
    RC = T(F32, 16, 32)
    RD = T(F32, 16, 32)
    QS = [T(BF16, 1024) for _ in range(2)]
    QKTs = [[T(BF16, 1024) for _ in range(2)] for _ in range(2)]
    VAs = [T(BF16, 8, 66) for _ in range(2)]
    SOs = [T(BF16, 512) for _ in range(2)]
    QNs = T(BF16, 1024)
    QKNT = [T(BF16, 1024) for _ in range(2)]
    VNs = [T(BF16, 8, 66) for _ in range(2)]
    wstg = [T(F32, DIN) for _ in range(2)]
    for k in range(8):
        load_cast(g, WIN[:, k, :], g.w_in[l, k * 128:(k + 1) * 128, :], wstg, k)
    P.dma("sp", ROPE, g.rope.rearrange("p (a b c) -> p a b c", a=2, b=32))
    for b in range(2):
        op_memset(P, "pool", VAs[b], 1.0)
        op_memset(P, "pool", VNs[b], 1.0)
    P.dma("sp", xt[0], src_tile(g, l, 0))

    def stage1(t):
        b = t % 2
        rms_to_hT(g, xt[b], 1 if t < 2 else 0, 0, hT2[b], junk, ss2[b], xn2[b], bank(0, BF16))

    stage1(0)
    for t in range(NT):
        b = t % 2
        s = 1 if t < 2 else 0
        hT = hT2[b]
        if t + 1 < NT:
            P.dma("sp", xt[1 - b], src_tile(g, l, t + 1))
        pbank = {7: 6, 0: 2, 1: 3, 2: 4, 3: 5, 4: 4, 5: 5, 6: 2}

        def piece(n):
            w = 512 if n < 7 else 32
            pb = bank(pbank[n])[:, 0:w]
            for k in range(8):
                op_mm(P, pb, hT[:, k, :], WIN[:, k, n * 512:n * 512 + w], start=(k == 0), stop=(k == 7))
            return pb

        pg = piece(7)
        op_tt(P, "dve", GT, pg, g.bgbc, ALU.add)
        op_act(P, E1, GT[:, 16:32], AF.Exp, scale=-1.0)
        op_act(P, L1, E1, AF.Ln, bias=1.0)
        CB = bank(6)[:, 64:112]
        P.op("pe", lambda e, CB=CB: e.matmul(CB[:, 0:8], g.trifw, L1[:, 0:8], start=True, stop=True),
             [g.trifw, L1[:, 0:8]], [CB[:, 0:8]])
        P.op("pe", lambda e, CB=CB: e.matmul(CB[:, 8:16], g.tribw, L1[:, 8:16], start=True, stop=True),
             [g.tribw, L1[:, 8:16]], [CB[:, 8:16]])
        P.op("pe", lambda e, CB=CB: e.matmul(CB[:, 16:32], g.ones, L1, start=True, stop=True),
             [g.ones, L1], [CB[:, 16:32]])
        op_act(P, SCQK[:, :, 0:8], CB[:, 0:16].rearrange("p (a b) -> p a b", a=2), AF.Exp,
               scale=-1.0, bias=math.log(0.125))
        op_tt(P, "dve", TG, GT[:, 0:16], CB[:, 0:16], ALU.add)
        op_act(P, SCQK[:, :, 8:16], TG.rearrange("p (a b) -> p a b", a=2), AF.Exp)
        op_act(P, g.Eend[:, t, :], CB[:, 16:32], AF.Exp, scale=-1.0)
        piece(0)
        piece(1)
        qk = g.PS[:, 1024:2048]
        if t >= 2:
            qk3 = qk.rearrange("p (h e) -> p h e", h=16)
            x1, x2 = qk3[:, :, 0:32], qk3[:, :, 32:64]
            cos = bc_mid(ROPE[:, 0, t - 2, :], 16)
            sin = bc_mid(ROPE[:, 1, t - 2, :], 16)
            op_tt(P, "dve", RA, x1, cos, ALU.mult)
            op_tt(P, "dve", RB, x2, sin, ALU.mult)
            op_tt(P, "dve", QKR[:, :, 0:32], RA, RB, ALU.subtract)
            op_tt(P, "dve", RC, x1, sin, ALU.mult)
            op_tt(P, "dve", RD, x2, cos, ALU.mult)
            op_tt(P, "dve", QKR[:, :, 32:64], RC, RD, ALU.add)
        else:
            op_cp(P, "act", QKR.rearrange("p a b -> p (a b)"), qk)
        for d in range(2):
            sc = bc_last(SCQK[:, d, :], 64)
            op_tt(P, "dve", QS[d].rearrange("p (a b) -> p a b", a=16), QKR, sc, ALU.mult)
        pv = piece(2)
        op_cp(P, "act", VAs[b][:, :, 0:64], pv.rearrange("p (a b) -> p a b", a=8))
        P.dma("sp", g.VA[t], VAs[b].rearrange("p a b -> p (a b)"))
        po = piece(3)
        if not (last and t < 2):
            op_act(P, SOs[b], po, AF.Sigmoid)
            P.dma("sp", g.SO[t], SOs[b])
        if t + 1 < NT:
            stage1(t + 1)
        piece(4)
        piece(5)
        op_act(P, QNs[:, 0:512], g.PS[:, 2048:2560], AF.Copy, scale=0.125)
        op_cp(P, "dve", QNs[:, 512:1024], g.PS[:, 2560:3072])
        ptb = bank(1, BF16)
        for j in range(8):
            op_tr(P, ptb[:, j * 128:(j + 1) * 128], QNs[:, j * 128:(j + 1) * 128], g.identJb if j < 4 else g.identb)
        op_cp(P, "dve", QKNT[b], ptb)
        P.dma("sp", g.QN[t], QKNT[b][:, 0:512])
        P.dma("sp", g.KN[t], QKNT[b][:, 512:1024])
        pn = piece(6)
        op_cp(P, "act", VNs[b][:, :, 0:64], pn.rearrange("p (a b) -> p a b", a=8))
        P.dma("sp", g.VN[t], VNs[b].rearrange("p a b -> p (a b)"))
        for d in range(2):
            ptb = bank(1 if d == 0 else 7, BF16)
            for j in range(8):
                op_tr(P, ptb[:, j * 128:(j + 1) * 128], QS[d][:, j * 128:(j + 1) * 128], g.identb)
            op_cp(P, "dve" if d == 0 else "act", QKTs[d][b], ptb)
            P.dma("sp", g.QKT[d][t], QKTs[d][b])
            P.dma("sp", g.KT[d][t], QS[d][:, 512:1024])


def phase_b(g, l):
    P, T, bank = g.P, g.T, g.bank
    g.top = g.mark
    last = (l == g.depth - 1) and not g.nolast
    BT = T(BF16, NPAT, 8, 128)
    KNall = T(BF16, NT, 512)
    VNall = T(BF16, NT, 528)
    NAM = T(F32, NPAT, 128)
    STG = [T(F32, 8, 128) for _ in range(2)]
    P.dma("sp", NAM, g.nam.rearrange("p (a b) -> p a b", a=NPAT))
    for b_ in range(2):
        op_memset(P, "pool", STG[b_], 0.0)
    setup = []

    def mk_pat(pi, dl):
        def f():
            stg = STG[pi % 2]
            for krl in range(2):
                for qrl in range(2):
                    a_ = 2 * dl + krl - qrl + 7
                    if a_ < 0 or a_ > 14:
                        continue
                    src = cap(g.rpbpad, [[1, 64], [15 * 128, 8], [1, 64]], off=(l * 120 + a_) * 128)
                    P.dma("sp", stg[krl * 64:(krl + 1) * 64, :, qrl * 64:(qrl + 1) * 64], src)
            op_tt(P, "dve", BT[:, pi, :, :], stg, bc_mid(NAM[:, pi, :], 8), ALU.add)
        return f

    def mk_kv(t0):
        def f():
            P.dma("sp", KNall[:, t0:t0 + 2, :], g.KN[t0:t0 + 2].rearrange("t p f -> p t f"))
            P.dma("sp", VNall[:, t0:t0 + 2, :], g.VN[t0:t0 + 2].rearrange("t p f -> p t f"))
        return f

    for pi, (jc, dl) in enumerate(g.pats):
        setup.append(mk_pat(pi, dl))
    for t0 in range(0, NT, 2):
        setup.append(mk_kv(t0))
    QK = [[T(BF16, 2, 4, 128) for _ in range(2)] for _ in range(2)]
    KTb = [[T(BF16, 512) for _ in range(2)] for _ in range(2)]
    VAb = [[T(BF16, 8, 66) for _ in range(2)] for _ in range(2)]
    ST = [T(BF16, 8, 128) for _ in range(2)]
    CF = [T(F32, 8, 66) for _ in range(2)]
    C16 = [T(BF16, 8, 66) for _ in range(2)]
    DEN = [T(F32, 8) for _ in range(2)]
    HB = [[T(F32, 8, 64) for _ in range(2)] for _ in range(2)]
    for d in range(2):
        op_memset(P, "pool", CF[d], 0.0)
        op_memset(P, "pool", C16[d], 0.0)
    order = [list(range(NT)), [1, 0] + list(range(NT - 1, 1, -1))]

    def loads(d, i):
        t = order[d][i]
        b = i % 2
        P.dma("sp", QK[d][b].rearrange("p a b c -> p (a b c)"), g.QKT[d][t])
        P.dma("sp", KTb[d][b], g.KT[d][t])
        P.dma("sp", VAb[d][b].rearrange("p a b -> p (a b)"), g.VA[t])

    for d in range(2):
        loads(d, 0)
    PSf = g.PS[:, :]
    pstr = int(PSf.ap[0][0])
    sbase_ = [1024, 2048]
    nb = 0
    dcb = 3072

    def part1(d, i):
        t = order[d][i]
        b = i % 2
        if i + 1 < NT:
            loads(d, i + 1)
        if setup:
            setup.pop(0)()
        qk, kt, va = QK[d][b], KTb[d][b], VAb[d][b]
        need_out = not (last and t < 2)
        sbase = sbase_[d]
        if need_out:
            for h in range(8):
                hp, base = h // 2, (h % 2) * 64
                so = sbase + (h % 2) * 512 + hp * 128
                op_mm(P, PSf[:, so: so + 128], qk[base:base + 64, 1, hp, :], qk[base:base + 64, 0, hp, :])
        for h in range(8):
            hp = h // 2
            o = PSf[:, dcb + (h // 4) * 512 + (h % 4) * 66: dcb + (h // 4) * 512 + (h % 4) * 66 + 66]
            op_mm(P, o, kt[:, hp * 128:(hp + 1) * 128], va[:, h, :])
        if need_out:
            for j in range(2):
                Sv = PSf[:, sbase + j * 512: sbase + (j + 1) * 512].rearrange("p (h t) -> p h t", h=4)
                op_tt(P, "dve", ST[d][:, j::2, :], Sv, bc_mid(g.maskb[d], 4), ALU.mult)
        for half in range(2):
            dcv = PSf[:, dcb + half * 512: dcb + half * 512 + 264].rearrange("p (h e) -> p h e", h=4)
            cf = CF[d][:, half * 4:(half + 1) * 4, :]
            ee = bc_last(g.Eend[:, t, d * 8 + half * 4: d * 8 + half * 4 + 4], 66)
            op_tt(P, "dve", cf, cf, dcv, ALU.add)
            op_tt(P, "dve", cf, cf, ee, ALU.mult)

    def part2(d, i):
        t = order[d][i]
        b = i % 2
        qk, va = QK[d][b], VAb[d][b]
        need_out = not (last and t < 2)
        if need_out:
            for h in range(8):
                hp, base = h // 2, (h % 2) * 64
                oo = nb + (h % 2) * 512 + hp * 66
                o = PSf[:, oo: oo + 65]
                op_mm(P, o, qk[base:base + 64, 0, hp, :], C16[d][base:base + 64, h, 0:65], start=True, stop=False)
                op_mm(P, o, ST[d][:, h, :], va[:, h, 0:65], start=False, stop=True)
        for half in range(2):
            op_cp(P, "dve", C16[d][:, half * 4:(half + 1) * 4, :], CF[d][:, half * 4:(half + 1) * 4, :])
        if need_out:
            for half in range(2):
                nv = PSf[:, nb + half * 512: nb + half * 512 + 264].rearrange("p (h e) -> p h e", h=4)
                dn = DEN[d][:, half * 4:(half + 1) * 4]
                op_act(P, dn, nv[:, :, 64], AF.Abs)
                op_ts(P, "dve", dn, dn, 1.0, None, ALU.max)
                P.op("dve", lambda e, dn=dn: e.reciprocal(dn, dn), [dn], [dn])
                op_tt(P, "dve", HB[d][b][:, half::2, :], nv[:, :, 0:64], bc_last(dn, 64), ALU.mult)
            P.dma("sp", g.H[d][t], HB[d][b].rearrange("p a b -> p (a b)"))

    for i in range(NT):
        part1(0, i)
        part1(1, i)
        part2(0, i)
        part2(1, i)
    while setup:
        setup.pop(0)()


def phase_c(g, l):
    P, T, bank = g.P, g.T, g.bank
    g.top = g.mark
    last = (l == g.depth - 1) and not g.nolast
    BT = T(BF16, NPAT, 8, 128)
    KNall = T(BF16, NT, 512)
    VNall = T(BF16, NT, 528)
    QNb = [T(BF16, 512) for _ in range(2)]
    PTb = [T(BF16, 7, 128) for _ in range(2)]
    RD = T(F32, 8)
    WO = T(BF16, 8, 1024)
    top_c = g.top
    wstg = [T(F32, 1024) for _ in range(2)]
    g.top = top_c
    xt = [T(F32, 1024) for _ in range(2)]
    Yt = [T(BF16, 1024) for _ in range(2)]
    HFb = [T(F32, 8, 64) for _ in range(2)]
    HBb = [T(F32, 8, 64) for _ in range(2)]
    SOb = [T(BF16, 512) for _ in range(2)]
    yT = T(BF16, 8, 128)
    st8 = T(F32, 4, 8)
    TMP = T(F32, 1024)
    HS = T(F32, 8, 64)
    for k in range(8):
        load_cast(g, WO[:, k, :], g.w_out[l, k * 128:(k + 1) * 128, :], wstg, k)
    qtiles = list(range(2, NT)) + ([] if last else [0, 1])
    PSf = g.PS[:, :]

    def loads(qi):
        t_ = qtiles[qi]
        b_ = qi % 2
        P.dma("sp", QNb[b_], g.QN[t_])
        P.dma("sp", xt[b_], src_tile(g, l, t_))
        P.dma("sp", HFb[b_].rearrange("p a b -> p (a b)"), g.H[0][t_])
        P.dma("sp", HBb[b_].rearrange("p a b -> p (a b)"), g.H[1][t_])
        P.dma("sp", SOb[b_], g.SO[t_])

    cur_s = None
    loads(0)
    for qi, t in enumerate(qtiles):
        b = qi % 2
        s = 1 if t < 2 else 0
        if s != cur_s:
            load_gates(g, l, s)
            cur_s = s
        if qi + 1 < len(qtiles):
            loads(qi + 1)
        op_tt(P, "dve", HS, HFb[b], HBb[b], ALU.add)
        P.op("dve", lambda e: e.reduce_sum(st8[:, 0, :], HS, AX.X), [HS], [st8[:, 0, :]])
        op_ts(P, "dve", st8[:, 1, :], st8[:, 0, :], 1.0 / 64, None, ALU.mult)
        op_tt(P, "dve", HS, HS, bc_last(st8[:, 1, :], 64), ALU.subtract)
        TM3 = TMP[:, 0:512].rearrange("p (a b) -> p a b", a=8)
        op_tt(P, "dve", TM3, HS, HS, ALU.mult)
        P.op("dve", lambda e: e.reduce_sum(st8[:, 2, :], TM3, AX.X), [TM3], [st8[:, 2, :]])
        op_ts(P, "dve", st8[:, 3, :], st8[:, 2, :], 1.0 / 64, EPS, ALU.mult, ALU.add)
        op_act(P, st8[:, 2, :], st8[:, 3, :], AF.Ln)
        op_act(P, st8[:, 3, :], st8[:, 2, :], AF.Exp, scale=-0.5)
        op_tt(P, "dve", HS, HS, bc_last(st8[:, 3, :], 64), ALU.mult)
        HS2 = HS.rearrange("p a b -> p (a b)")
        op_tt(P, "dve", HS2, HS2, g.ngbc, ALU.mult)
        op_tt(P, "dve", Yt[b][:, 0:512], HS2, SOb[b], ALU.mult)
        if t >= 2:
            keys = [(tk + 2, pt) for (tk, pt) in g.nakeys[t - 2]] + [(0, None), (1, None)]
        else:
            keys = [(0, None), (1, None)]
        nk = len(keys)
        def emit_qk(h):
            hp, base = h // 2, (h % 2) * 64
            sb = 1024 if h % 2 == 0 else 2048
            for i, (tk, pt) in enumerate(keys):
                o = PSf[:, sb + i * 128: sb + (i + 1) * 128]
                op_mm(P, o, KNall[base:base + 64, tk, hp * 128:(hp + 1) * 128],
                      QNb[b][base:base + 64, hp * 128:(hp + 1) * 128], start=True, stop=(pt is None))
                if pt is not None:
                    op_mm(P, o, g.identb, BT[:, pt, h, :], start=False, stop=True)

        emit_qk(0)
        for h in range(8):
            sb = 1024 if h % 2 == 0 else 2048
            if h + 1 < 8:
                emit_qk(h + 1)
            pt_ = PTb[h % 2]
            op_act(P, pt_[:, 0:nk, :], PSf[:, sb: sb + nk * 128].rearrange("p (a b) -> p a b", a=nk), AF.Exp)
            o = PSf[:, 3072 + (h // 4) * 512 + (h % 4) * 66: 3072 + (h // 4) * 512 + (h % 4) * 66 + 65]
            for i, (tk, pt) in enumerate(keys):
                op_mm(P, o, pt_[:, i, :], VNall[:, tk, h * 66:h * 66 + 65], start=(i == 0), stop=(i == nk - 1))
        for half in range(2):
            nv = PSf[:, 3072 + half * 512: 3072 + half * 512 + 264].rearrange("p (h e) -> p h e", h=4)
            dn = RD[:, half * 4:(half + 1) * 4]
            P.op("dve", lambda e, dn=dn, nv=nv: e.reciprocal(dn, nv[:, :, 64]), [nv[:, :, 64]], [dn])
            yn = Yt[b][:, 512 + half * 256: 512 + (half + 1) * 256].rearrange("p (a b) -> p a b", a=4)
            op_tt(P, "dve", yn, nv[:, :, 0:64], bc_last(dn, 64), ALU.mult)
        x_ = xt[b]
        ptb = bank(0, BF16)
        for c in range(8):
            op_tr(P, ptb[:, c * 128:(c + 1) * 128], Yt[b][:, c * 128:(c + 1) * 128], g.identb if c < 4 else g.identJb)
        op_cp(P, "dve", yT.rearrange("p a b -> p (a b)"), ptb)
        for n in range(2):
            pb = bank(1 - n)
            for k in range(8):
                op_mm(P, pb, yT[:, k, :], WO[:, k, n * 512:(n + 1) * 512], start=(k == 0), stop=(k == 7))
            op_tt(P, "dve", TMP[:, n * 512:(n + 1) * 512], pb, g.G1[:, n * 512:(n + 1) * 512], ALU.mult)
        op_tt(P, "dve", x_, x_, TMP, ALU.add)
        P.dma("sp", g.xs[t], x_)


def phase_d1(g, l):
    P, T, bank = g.P, g.T, g.bank
    g.top = g.mark
    last = (l == g.depth - 1) and not g.nolast
    WO = T(BF16, 8, 1024)
    xt = [T(F32, 1024) for _ in range(2)]
    Yt = [T(BF16, 1024) for _ in range(2)]
    HFb = [T(F32, 8, 64) for _ in range(2)]
    HBb = [T(F32, 8, 64) for _ in range(2)]
    SOb = [T(BF16, 512) for _ in range(2)]
    yT = T(BF16, 8, 128)
    st8 = T(F32, 4, 8)
    TMP = T(F32, 1024)
    HS = T(F32, 8, 64)
    wstg = [T(F32, 1024) for _ in range(2)]
    for k in range(8):
        load_cast(g, WO[:, k, :], g.w_out[l, k * 128:(k + 1) * 128, :], wstg, k)
    tiles = list(range(NT)) if not last else list(range(2, NT))
    PSf = g.PS[:, :]

    def loads(i):
        t = tiles[i]
        b = i % 2
        P.dma("sp", xt[b], src_tile(g, l, t))
        P.dma("sp", HFb[b].rearrange("p a b -> p (a b)"), g.H[0][t])
        P.dma("sp", HBb[b].rearrange("p a b -> p (a b)"), g.H[1][t])
        P.dma("sp", SOb[b], g.SO[t])
        P.dma("sp", Yt[b][:, 512:1024], g.Y[t, :, 512:1024])

    cur_s = None
    loads(0)
    for i, t in enumerate(tiles):
        b = i % 2
        s = 1 if t < 2 else 0
        if s != cur_s:
            load_gates(g, l, s)
            cur_s = s
        if i + 1 < len(tiles):
            loads(i + 1)
        x_ = xt[b]
        op_tt(P, "dve", HS, HFb[b], HBb[b], ALU.add)
        P.op("dve", lambda e: e.reduce_sum(st8[:, 0, :], HS, AX.X), [HS], [st8[:, 0, :]])
        op_ts(P, "dve", st8[:, 1, :], st8[:, 0, :], 1.0 / 64, None, ALU.mult)
        op_tt(P, "dve", HS, HS, bc_last(st8[:, 1, :], 64), ALU.subtract)
        TM3 = TMP[:, 0:512].rearrange("p (a b) -> p a b", a=8)
        op_tt(P, "dve", TM3, HS, HS, ALU.mult)
        P.op("dve", lambda e: e.reduce_sum(st8[:, 2, :], TM3, AX.X), [TM3], [st8[:, 2, :]])
        op_ts(P, "dve", st8[:, 3, :], st8[:, 2, :], 1.0 / 64, EPS, ALU.mult, ALU.add)
        op_act(P, st8[:, 2, :], st8[:, 3, :], AF.Ln)
        op_act(P, st8[:, 3, :], st8[:, 2, :], AF.Exp, scale=-0.5)
        op_tt(P, "dve", HS, HS, bc_last(st8[:, 3, :], 64), ALU.mult)
        HS2 = HS.rearrange("p a b -> p (a b)")
        op_tt(P, "dve", HS2, HS2, g.ngbc, ALU.mult)
        op_tt(P, "dve", Yt[b][:, 0:512], HS2, SOb[b], ALU.mult)
        ptb = bank(0, BF16)
        for c in range(8):
            op_tr(P, ptb[:, c * 128:(c + 1) * 128], Yt[b][:, c * 128:(c + 1) * 128], g.identb if c < 4 else g.identJb)
        op_cp(P, "dve", yT.rearrange("p a b -> p (a b)"), ptb)
        for n in range(2):
            pb = bank(2 + n)
            for k in range(8):
                op_mm(P, pb, yT[:, k, :], WO[:, k, n * 512:(n + 1) * 512], start=(k == 0), stop=(k == 7))
            op_tt(P, "dve", TMP[:, n * 512:(n + 1) * 512], pb, g.G1[:, n * 512:(n + 1) * 512], ALU.mult)
        op_tt(P, "dve", x_, x_, TMP, ALU.add)
        P.dma("sp", g.xs[t], x_)


def phase_d2(g, l):
    P, T, bank = g.P, g.T, g.bank
    g.top = g.mark
    last = (l == g.depth - 1) and not g.nolast
    W1 = T(BF16, 8, DFF)
    W2 = T(BF16, 32, 1024)
    xt = [T(F32, 1024) for _ in range(2)]
    junk = T(BF16, 1024)
    ss2 = [T(F32, 4) for _ in range(2)]
    xn2 = [T(BF16, 1024) for _ in range(2)]
    hT2 = [T(BF16, 8, 128) for _ in range(2)]
    ss = T(F32, 4)
    top0 = g.top
    wstg = [T(F32, 2048) for _ in range(2)]
    g.top = top0
    U = T(BF16, DFF)
    uT = T(BF16, 32, 128)
    R = [T(BF16, 512) for _ in range(2)]
    TMP = T(F32, 1024)
    if last:
        g.FG = T(F32, 1024)
        P.dma("sp", g.FG, cap(g.final_g, [[0, 128], [1, 1024]]))
    for k in range(16):
        load_cast(g, W1[:, k // 2, (k % 2) * 2048:(k % 2 + 1) * 2048],
                  g.w1[l, (k // 2) * 128:(k // 2 + 1) * 128, (k % 2) * 2048:(k % 2 + 1) * 2048], wstg, k)
    for k in range(16):
        load_cast(g, W2[:, k * 2:(k + 1) * 2, :],
                  g.w2[l, k * 256:(k + 1) * 256, :].rearrange("(k p) n -> p k n", p=128), wstg, k)
    tiles = list(range(NT)) if not last else list(range(2, NT))
    cur_s = None
    P.dma("sp", xt[0], g.xs[tiles[0]])

    def stage1(i):
        t_ = tiles[i]
        rms_to_hT(g, xt[i % 2], 1 if t_ < 2 else 0, 1, hT2[i % 2], junk, ss2[i % 2], xn2[i % 2], bank(1, BF16))

    stage1(0)
    for i, t in enumerate(tiles):
        b = i % 2
        s = 1 if t < 2 else 0
        if s != cur_s:
            load_gates(g, l, s)
            cur_s = s
        if i + 1 < len(tiles):
            P.dma("sp", xt[1 - b], g.xs[tiles[i + 1]])
        x_ = xt[b]
        hT = hT2[b]
        for m in range(8):
            pb = bank(4 + (m % 4))
            for k in range(8):
                op_mm(P, pb, hT[:, k, :], W1[:, k, m * 512:(m + 1) * 512], start=(k == 0), stop=(k == 7))
            r = R[m % 2]
            op_act(P, r, pb, AF.Relu)
            op_tt(P, "dve", U[:, m * 512:(m + 1) * 512], r, r, ALU.mult)
        if i + 1 < len(tiles):
            stage1(i + 1)
        for q in range(4):
            ptb = bank(q % 2, BF16)
            for c in range(8):
                cc_ = q * 8 + c
                op_tr(P, ptb[:, c * 128:(c + 1) * 128], U[:, cc_ * 128:(cc_ + 1) * 128], g.identb)
            op_cp(P, "dve", uT[:, q * 8:(q + 1) * 8, :].rearrange("p a b -> p (a b)"), ptb)
        for n in range(2):
            pb = bank(2 + n)
            for m in range(32):
                op_mm(P, pb, uT[:, m, :], W2[:, m, n * 512:(n + 1) * 512], start=(m == 0), stop=(m == 31))
            op_tt(P, "dve", TMP[:, n * 512:(n + 1) * 512], pb, g.G2[:, n * 512:(n + 1) * 512], ALU.mult)
        op_tt(P, "dve", x_, x_, TMP, ALU.add)
        if not last:
            P.dma("sp", g.xs[t], x_)
        else:
            op_act(P, junk, x_, AF.Square, accum_out=ss[:, 0:1])
            op_ts(P, "dve", ss[:, 1:2], ss[:, 0:1], 1.0 / D, EPS, ALU.mult, ALU.add)
            op_act(P, ss[:, 3:4], ss[:, 1:2], AF.Ln)
            op_act(P, ss[:, 2:3], ss[:, 3:4], AF.Exp, scale=-0.5)
            op_tsm(P, "dve", TMP, x_, ss[:, 2:3])
            op_tt(P, "dve", TMP, TMP, g.FG, ALU.mult)
            P.dma("sp", g.out[(t - 2) * 128:(t - 1) * 128, :], TMP)


_CACHE = {}


def make_in_maps(inputs):
    cst, rope, nam = host_constants()
    f = lambda a: np.ascontiguousarray(np.asarray(a, dtype=np.float32))
    shared = {
        "w_ada": f(inputs["w_ada"]), "b_ada": f(inputs["b_ada"]), "norm1_g": f(inputs["norm1_g"]),
        "w_in": f(inputs["w_in"]), "b_gate": f(inputs["b_gate"]), "mlstm_norm_g": f(inputs["mlstm_norm_g"]),
        "rpb": f(inputs["rpb"]).reshape(DEPTH, 120, 31), "w_out": f(inputs["w_out"]),
        "norm2_g": f(inputs["norm2_g"]), "w_mlp1": f(inputs["w_mlp1"]), "w_mlp2": f(inputs["w_mlp2"]),
        "final_g": f(inputs["final_g"]).reshape(1, D), "cst": cst, "rope": rope, "nam": nam,
    }
    x = f(inputs["x"])
    c = f(inputs["c"])
    ctx = f(inputs["ctx"])
    c_ctx = f(inputs["c_ctx"])
    maps = []
    for b in range(8):
        m = dict(shared)
        m["x"] = x[b]
        m["ctx"] = ctx[b]
        m["cc"] = np.ascontiguousarray(np.stack([c[b], c_ctx], axis=0))
        maps.append(m)
    return maps


def kernel(**inputs):
    if "nc" not in _CACHE:
        _CACHE["nc"] = build_program()
    nc = _CACHE["nc"]
    maps = make_in_maps(inputs)
    res = run_bass_kernel_spmd(nc, maps, core_ids=list(range(8)))
    out = np.stack([np.asarray(res.results[b]["out"], dtype=np.float32) for b in range(8)], axis=0)
    return out
```
